# Optimizing a Trainium2 kernel written in Bass

```python
import math
import jax, jax.numpy as jnp
from jax import lax
import numpy as np

D_MODEL = 1024
BATCH = 8
SEQ = 2048
DEPTH = 4

CTX_LEN = 256
GRID_W = 64
N_MIXERS = 4
D_FF = 4 * D_MODEL
NORM_EPS = 1e-6
ROPE_THETA = 10000.0
CONV_W = 4
CONV_PAD = (2, 1)
CHUNK = 64

GDN_QK_HEADS = 8
GDN_V_HEADS = 16
GDN_HEAD_DIM = 128
GDN_KEY_DIM = GDN_QK_HEADS * GDN_HEAD_DIM
GDN_VALUE_DIM = GDN_V_HEADS * GDN_HEAD_DIM
GDN_CONV_DIM = 2 * GDN_KEY_DIM + GDN_VALUE_DIM
GDN_IN_DIM = GDN_CONV_DIM + GDN_VALUE_DIM + 4 * GDN_V_HEADS

RET_HEADS = 4
RET_DK = 256
RET_DV = 512
RET_KEY_DIM = RET_HEADS * RET_DK
RET_VALUE_DIM = RET_HEADS * RET_DV
RET_IN_DIM = 2 * RET_KEY_DIM + 2 * RET_VALUE_DIM

LRU_WIDTH = 1280
LRU_BLOCKS = 10
LRU_BLOCK_W = LRU_WIDTH // LRU_BLOCKS
LRU_C = 8.0

SWA_Q_HEADS = 16
SWA_KV_HEADS = 4
SWA_GROUP = SWA_Q_HEADS // SWA_KV_HEADS
SWA_HEAD_DIM = 64
SWA_WINDOW = 128
SWA_BLOCK = 128
SWA_Q_DIM = SWA_Q_HEADS * SWA_HEAD_DIM
SWA_KV_DIM = SWA_KV_HEADS * SWA_HEAD_DIM
SWA_IN_DIM = SWA_Q_DIM + 2 * SWA_KV_DIM

kernel_name = 'hybrid_interleaved_diffusion_trunk'


def n_uses(kind):
    return (DEPTH - kind + N_MIXERS - 1) // N_MIXERS


def rmsnorm(x, g):
    xf = x.astype(jnp.float32)
    y = xf * lax.rsqrt(jnp.mean(xf * xf, axis=-1, keepdims=True) + NORM_EPS)
    return (y * g.astype(jnp.float32)).astype(x.dtype)


def l2norm(x):
    xf = x.astype(jnp.float32)
    return (xf * lax.rsqrt(jnp.sum(xf * xf, axis=-1, keepdims=True) + NORM_EPS)).astype(x.dtype)


def head_groupnorm(o, g):
    of = o.astype(jnp.float32)
    mu = jnp.mean(of, axis=-1, keepdims=True)
    var = jnp.mean(jnp.square(of - mu), axis=-1, keepdims=True)
    y = ((of - mu) * lax.rsqrt(var + NORM_EPS)).reshape(o.shape[0], o.shape[1], -1)
    return (y * g.astype(jnp.float32)).astype(o.dtype)


def adaln(cvec, w, b):
    return jnp.split(jax.nn.silu(cvec) @ w + b, 6, axis=-1)


def modulate(h, shift, scale):
    return h * (1.0 + scale) + shift


def sqrelu_mlp(h, w1, w2):
    return jnp.square(jax.nn.relu(h @ w1)) @ w2


def short_conv(x, w):
    return lax.conv_general_dilated(x, w[:, None, :].astype(x.dtype), window_strides=(1,),
                                    padding=[CONV_PAD], dimension_numbers=('NWC', 'WIO', 'NWC'),
                                    feature_group_count=x.shape[-1])


def axial_angles(rows, dh):
    t = jnp.arange(rows * GRID_W)
    row = (t // GRID_W).astype(jnp.float32)
    col = (t % GRID_W).astype(jnp.float32)
    nf = dh // 4
    inv = ROPE_THETA ** (-jnp.arange(nf, dtype=jnp.float32) / nf)
    return row[:, None] * inv, col[:, None] * inv


def axial_rope(x, ang):
    ang_r, ang_c = ang

    def rot(t, a):
        t1, t2 = jnp.split(t, 2, axis=-1)
        cos = jnp.cos(a)[None, :, None, :]
        sin = jnp.sin(a)[None, :, None, :]
        return jnp.concatenate([t1 * cos - t2 * sin, t2 * cos + t1 * sin], axis=-1)

    xr, xcol = jnp.split(x.astype(jnp.float32), 2, axis=-1)
    return jnp.concatenate([rot(xr, ang_r), rot(xcol, ang_c)], axis=-1).astype(x.dtype)


def split_tokens(y, n_ctx, need_ctx):
    return (y[:, :n_ctx], y[:, n_ctx:]) if need_ctx else (None, y)


def chunked_linear_attention(q, k, v, g, beta, s0):
    f32 = jnp.float32
    B, H, L, dk = q.shape
    dv = v.shape[-1]
    n = L // CHUNK

    def blocks(t):
        return t.astype(f32).reshape(B, H, n, CHUNK, *t.shape[3:])

    q, k, v, g = blocks(q), blocks(k), blocks(v), blocks(g)
    G = jnp.cumsum(g, axis=-1)
    idx = jnp.arange(CHUNK)
    incl = idx[:, None] >= idx[None, :]
    seg = G[..., :, None] - G[..., None, :]
    dec = jnp.where(incl, jnp.exp(jnp.where(incl, seg, 0.0)), 0.0)
    qk = jnp.einsum('bhncd,bhnsd->bhncs', q, k) * dec
    u = v
    extra = ()
    if beta is not None:
        beta = blocks(beta)
        strict = idx[:, None] > idx[None, :]
        a = jnp.where(strict, jnp.einsum('bhncd,bhnsd->bhncs', k, k) * dec * beta[..., :, None], 0.0)
        rhs = jnp.concatenate([k * (beta * jnp.exp(G))[..., None], v * beta[..., None]], axis=-1)
        sol = lax.linalg.triangular_solve(a + jnp.eye(CHUNK, dtype=f32), rhs, left_side=True,
                                          lower=True, unit_diagonal=True)
        extra = (sol[..., :dk],)
        u = sol[..., dk:]
    q_dec = q * jnp.exp(G)[..., None]
    k_dec = k * jnp.exp(G[..., -1:] - G)[..., None]
    g_tot = jnp.exp(G[..., -1])
    xs = tuple(jnp.moveaxis(t, 2, 0) for t in (q_dec, qk, u, k_dec, g_tot) + extra)

    def step(s, xs_n):
        qd, qkn, un, kd, gt = xs_n[:5]
        if beta is not None:
            un = un - jnp.einsum('bhcd,bhde->bhce', xs_n[5], s)
        o = jnp.einsum('bhcd,bhde->bhce', qd, s) + jnp.einsum('bhcs,bhse->bhce', qkn, un)
        s = s * gt[..., None, None] + jnp.einsum('bhcd,bhce->bhde', kd, un)
        return s, o

    s_fin, o = lax.scan(step, s0.astype(f32), xs)
    return jnp.moveaxis(o, 0, 2).reshape(B, H, L, dv), s_fin


def bidirectional_prefix_scan(q, k, v, g, beta, n_ctx):
    B, H, _, dk = q.shape
    dv = v.shape[-1]

    def piece(t, side, rev):
        if t is None:
            return None
        t = t[:, :, :n_ctx] if side == 0 else t[:, :, n_ctx:]
        return jnp.flip(t, axis=2) if rev else t

    o_ctx, o_lat = 0.0, 0.0
    for d in range(2):
        rev = d == 1
        b_d = None if beta is None else beta[d]
        args = (q, k, v, g[d], b_d)
        o_c, s_c = chunked_linear_attention(*[piece(t, 0, rev) for t in args],
                                            jnp.zeros((B, H, dk, dv), jnp.float32))
        o_l, _ = chunked_linear_attention(*[piece(t, 1, rev) for t in args], s_c)
        if rev:
            o_c, o_l = jnp.flip(o_c, axis=2), jnp.flip(o_l, axis=2)
        o_ctx = o_ctx + o_c
        o_lat = o_lat + o_l
    return jnp.concatenate([o_ctx, o_lat], axis=2).astype(v.dtype)


def gdn_mixer(hc, hl, w_in, conv_w, a_log, dt_bias, norm_g, w_out, need_ctx):
    n_ctx = hc.shape[1]
    proj = jnp.concatenate([hc, hl], axis=1) @ w_in
    B, T, _ = proj.shape
    qkv = proj[..., :GDN_CONV_DIM]
    z = proj[..., GDN_CONV_DIM:GDN_CONV_DIM + GDN_VALUE_DIM]
    ab = proj[..., GDN_CONV_DIM + GDN_VALUE_DIM:].reshape(B, T, 2, 2, GDN_V_HEADS)
    qkv = jnp.concatenate([jax.nn.silu(short_conv(qkv[:, :n_ctx], conv_w)),
                           jax.nn.silu(short_conv(qkv[:, n_ctx:], conv_w))], axis=1)
    rep = GDN_V_HEADS // GDN_QK_HEADS
    q = l2norm(qkv[..., :GDN_KEY_DIM].reshape(B, T, GDN_QK_HEADS, GDN_HEAD_DIM)) * GDN_HEAD_DIM ** -0.5
    k = l2norm(qkv[..., GDN_KEY_DIM:2 * GDN_KEY_DIM].reshape(B, T, GDN_QK_HEADS, GDN_HEAD_DIM))
    v = qkv[..., 2 * GDN_KEY_DIM:].reshape(B, T, GDN_V_HEADS, GDN_HEAD_DIM)
    q = jnp.repeat(q, rep, axis=2).transpose(0, 2, 1, 3)
    k = jnp.repeat(k, rep, axis=2).transpose(0, 2, 1, 3)
    v = v.transpose(0, 2, 1, 3)
    abf = ab.astype(jnp.float32)
    g = -jnp.exp(a_log.astype(jnp.float32)) * jax.nn.softplus(abf[:, :, :, 0] + dt_bias.astype(jnp.float32))
    beta = jax.nn.sigmoid(abf[:, :, :, 1])
    g = g.transpose(2, 0, 3, 1)
    beta = beta.transpose(2, 0, 3, 1)
    o = bidirectional_prefix_scan(q, k, v, g, beta, n_ctx).transpose(0, 2, 1, 3)
    if not need_ctx:
        o, z = o[:, n_ctx:], z[:, n_ctx:]
    z = z.reshape(o.shape)
    y = (rmsnorm(o, norm_g) * jax.nn.silu(z)).reshape(o.shape[0], o.shape[1], GDN_VALUE_DIM) @ w_out
    return split_tokens(y, n_ctx, need_ctx)


def retention_mixer(hc, hl, w_in, decay_logit, norm_g, w_out, ang, need_ctx):
    n_ctx = hc.shape[1]
    proj = jnp.concatenate([hc, hl], axis=1) @ w_in
    B, T, _ = proj.shape
    q = proj[..., :RET_KEY_DIM].reshape(B, T, RET_HEADS, RET_DK)
    k = proj[..., RET_KEY_DIM:2 * RET_KEY_DIM].reshape(B, T, RET_HEADS, RET_DK)
    v = proj[..., 2 * RET_KEY_DIM:2 * RET_KEY_DIM + RET_VALUE_DIM].reshape(B, T, RET_HEADS, RET_DV)
    gate = proj[..., 2 * RET_KEY_DIM + RET_VALUE_DIM:]
    q = jnp.concatenate([q[:, :n_ctx], axial_rope(q[:, n_ctx:], ang)], axis=1) * RET_DK ** -0.5
    k = jnp.concatenate([k[:, :n_ctx], axial_rope(k[:, n_ctx:], ang)], axis=1)
    q, k, v = (t.transpose(0, 2, 1, 3) for t in (q, k, v))
    log_gamma = jax.nn.log_sigmoid(decay_logit.astype(jnp.float32))
    g = jnp.broadcast_to(log_gamma[:, None, :, None], (2, B, RET_HEADS, T))
    o = bidirectional_prefix_scan(q, k, v, g, None, n_ctx).transpose(0, 2, 1, 3)
    if not need_ctx:
        o, gate = o[:, n_ctx:], gate[:, n_ctx:]
    y = (head_groupnorm(o, norm_g) * jax.nn.silu(gate)) @ w_out
    return split_tokens(y, n_ctx, need_ctx)


def lru_coeffs(x, w_gate, b_gate, lam):
    B, L, W = x.shape
    xb = x.reshape(B, L, LRU_BLOCKS, LRU_BLOCK_W)
    gates = jax.nn.sigmoid((jnp.einsum('blnc,nce->blne', xb, w_gate) + b_gate).astype(jnp.float32))
    r = gates[..., :LRU_BLOCK_W].reshape(B, L, W)
    i = gates[..., LRU_BLOCK_W:].reshape(B, L, W)
    log_a = -LRU_C * r * jax.nn.softplus(-lam.astype(jnp.float32))
    a = jnp.exp(log_a)
    b = jnp.sqrt(-jnp.expm1(2.0 * log_a)) * i * x.astype(jnp.float32)
    return a, b


def linear_scan(a, b, h0):
    def comb(l, r):
        return l[0] * r[0], r[0] * l[1] + r[1]
    a_cum, b_cum = lax.associative_scan(comb, (a, b), axis=1)
    h = a_cum * h0[:, None] + b_cum
    return h, h[:, -1]


def rglru_mixer(hc, hl, w_in, conv_w, conv_b, w_gate, b_gate, lam, w_out, need_ctx):
    n_ctx = hc.shape[1]
    proj = jnp.concatenate([hc, hl], axis=1) @ w_in
    B = proj.shape[0]
    xb, gb = proj[..., :LRU_WIDTH], proj[..., LRU_WIDTH:]
    xs_c = short_conv(xb[:, :n_ctx], conv_w) + conv_b
    xs_l = short_conv(xb[:, n_ctx:], conv_w) + conv_b
    h_ctx, h_lat = 0.0, 0.0
    for d in range(2):
        a_c, b_c = lru_coeffs(xs_c, w_gate[d], b_gate[d], lam[d])
        a_l, b_l = lru_coeffs(xs_l, w_gate[d], b_gate[d], lam[d])
        if d == 1:
            a_c, b_c, a_l, b_l = (jnp.flip(t, axis=1) for t in (a_c, b_c, a_l, b_l))
        hc_d, h_last = linear_scan(a_c, b_c, jnp.zeros((B, LRU_WIDTH), jnp.float32))
        hl_d, _ = linear_scan(a_l, b_l, h_last)
        if d == 1:
            hc_d, hl_d = jnp.flip(hc_d, axis=1), jnp.flip(hl_d, axis=1)
        h_ctx = h_ctx + hc_d
        h_lat = h_lat + hl_d
    h = jnp.concatenate([h_ctx, h_lat], axis=1).astype(proj.dtype)
    if not need_ctx:
        h, gb = h[:, n_ctx:], gb[:, n_ctx:]
    y = (h * jax.nn.gelu(gb)) @ w_out
    return split_tokens(y, n_ctx, need_ctx)


def sink_softmax(s, sink):
    sk = jnp.broadcast_to(sink.astype(jnp.float32)[None, :, :, None, None], s.shape[:-1] + (1,))
    return jax.nn.softmax(jnp.concatenate([s, sk], axis=-1), axis=-1)[..., :-1]


def swa_mixer(hc, hl, w_in, sink, w_out, ang, need_ctx):
    n_ctx = hc.shape[1]
    proj = jnp.concatenate([hc, hl], axis=1) @ w_in
    B, T, _ = proj.shape
    L = T - n_ctx
    q = proj[..., :SWA_Q_DIM].reshape(B, T, SWA_Q_HEADS, SWA_HEAD_DIM)
    k = proj[..., SWA_Q_DIM:SWA_Q_DIM + SWA_KV_DIM].reshape(B, T, SWA_KV_HEADS, SWA_HEAD_DIM)
    v = proj[..., SWA_Q_DIM + SWA_KV_DIM:].reshape(B, T, SWA_KV_HEADS, SWA_HEAD_DIM)
    kc, vc = k[:, :n_ctx], v[:, :n_ctx]
    ql = axial_rope(q[:, n_ctx:], ang).reshape(B, L, SWA_KV_HEADS, SWA_GROUP, SWA_HEAD_DIM)
    kl = axial_rope(k[:, n_ctx:], ang)
    vl = v[:, n_ctx:]
    scale = SWA_HEAD_DIM ** -0.5
    sink_kg = sink.reshape(SWA_KV_HEADS, SWA_GROUP)
    nb = L // SWA_BLOCK

    def band(t):
        tp = jnp.pad(t, ((0, 0), (SWA_BLOCK, SWA_BLOCK), (0, 0), (0, 0)))
        tp = tp.reshape(B, nb + 2, SWA_BLOCK, SWA_KV_HEADS, SWA_HEAD_DIM)
        return jnp.concatenate([tp[:, :-2], tp[:, 1:-1], tp[:, 2:]], axis=2)

    qi = jnp.arange(SWA_BLOCK)
    kj = jnp.arange(3 * SWA_BLOCK)
    rel = kj[None, :] - SWA_BLOCK - qi[:, None]
    key_pos = jnp.arange(nb)[:, None, None] * SWA_BLOCK + kj[None, None, :] - SWA_BLOCK
    mask = (jnp.abs(rel) <= SWA_WINDOW)[None] & (key_pos >= 0) & (key_pos < L)
    nloc = 3 * SWA_BLOCK

    def attend_block(xs):
        qb, kb, vb, mb = xs
        s_loc = jnp.einsum('bqkgd,bskd->bkgqs', qb, kb).astype(jnp.float32) * scale
        s_loc = jnp.where(mb, s_loc, -jnp.inf)
        s_ctx = jnp.einsum('bqkgd,bskd->bkgqs', qb, kc).astype(jnp.float32) * scale
        p = sink_softmax(jnp.concatenate([s_loc, s_ctx], axis=-1), sink_kg).astype(vb.dtype)
        return (jnp.einsum('bkgqs,bskd->bqkgd', p[..., :nloc], vb)
                + jnp.einsum('bkgqs,bskd->bqkgd', p[..., nloc:], vc))

    xs = (jnp.moveaxis(ql.reshape(B, nb, SWA_BLOCK, SWA_KV_HEADS, SWA_GROUP, SWA_HEAD_DIM), 1, 0),
          jnp.moveaxis(band(kl), 1, 0), jnp.moveaxis(band(vl), 1, 0), mask)
    o_l = lax.map(attend_block, xs)
    o_l = jnp.moveaxis(o_l, 0, 1).reshape(B, L, SWA_Q_DIM)
    if not need_ctx:
        return None, o_l @ w_out
    qc = q[:, :n_ctx].reshape(B, n_ctx, SWA_KV_HEADS, SWA_GROUP, SWA_HEAD_DIM)
    s_c = jnp.einsum('bqkgd,bskd->bkgqs', qc, kc).astype(jnp.float32) * scale
    p_c = sink_softmax(s_c, sink_kg).astype(vc.dtype)
    o_c = jnp.einsum('bkgqs,bskd->bqkgd', p_c, vc).reshape(B, n_ctx, SWA_Q_DIM)
    y = jnp.concatenate([o_c, o_l], axis=1) @ w_out
    return split_tokens(y, n_ctx, need_ctx)


def setup_inputs(seed: int = 0) -> dict:
    key = jax.random.key(seed)
    keys = iter(jax.random.split(key, 48))
    f32 = jnp.float32
    D = D_MODEL

    def nrm(shape, scale):
        return jax.random.normal(next(keys), shape, f32) * scale

    def unif(shape, lo, hi):
        return jax.random.uniform(next(keys), shape, f32, lo, hi)

    def gain(shape):
        return 1.0 + nrm(shape, 0.02)

    nA, nB, nC, nD = (n_uses(kind) for kind in range(N_MIXERS))
    dt = jnp.exp(unif((nA, 2, GDN_V_HEADS), math.log(1e-3), math.log(1e-1)))
    ret_gamma_logit = jnp.log(2.0 ** (5.0 + jnp.arange(RET_HEADS, dtype=f32)) - 1.0)
    lru_a = unif((nC, 2, LRU_WIDTH), 0.9, 0.999) ** (1.0 / LRU_C)
    return {
        'x': nrm((BATCH, SEQ, D), 1.0),
        'c': nrm((BATCH, D), 1.0),
        'ctx': nrm((BATCH, CTX_LEN, D), 1.0),
        'c_ctx': nrm((D,), 1.0),
        'norm_mix_g': gain((DEPTH, D)),
        'norm_ffn_g': gain((DEPTH, D)),
        'w_mod': nrm((DEPTH, D, 6 * D), 0.5 * D ** -0.5),
        'b_mod': nrm((DEPTH, 6 * D), 0.02),
        'w_ff1': nrm((DEPTH, D, D_FF), D ** -0.5),
        'w_ff2': nrm((DEPTH, D_FF, D), D_FF ** -0.5),
        'norm_out_g': gain((D,)),
        'gdn_w_in': nrm((nA, D, GDN_IN_DIM), D ** -0.5),
        'gdn_conv_w': nrm((nA, CONV_W, GDN_CONV_DIM), CONV_W ** -0.5),
        'gdn_a_log': jnp.log(unif((nA, 2, GDN_V_HEADS), 1.0, 16.0)),
        'gdn_dt_bias': dt + jnp.log(-jnp.expm1(-dt)),
        'gdn_norm_g': gain((nA, GDN_HEAD_DIM)),
        'gdn_w_out': nrm((nA, GDN_VALUE_DIM, D), GDN_VALUE_DIM ** -0.5),
        'ret_w_in': nrm((nB, D, RET_IN_DIM), D ** -0.5),
        'ret_decay_logit': ret_gamma_logit + nrm((nB, 2, RET_HEADS), 0.05),
        'ret_norm_g': gain((nB, RET_VALUE_DIM)),
        'ret_w_out': nrm((nB, RET_VALUE_DIM, D), RET_VALUE_DIM ** -0.5),
        'lru_w_in': nrm((nC, D, 2 * LRU_WIDTH), D ** -0.5),
        'lru_conv_w': nrm((nC, CONV_W, LRU_WIDTH), CONV_W ** -0.5),
        'lru_conv_b': nrm((nC, LRU_WIDTH), 0.02),
        'lru_w_gate': nrm((nC, 2, LRU_BLOCKS, LRU_BLOCK_W, 2 * LRU_BLOCK_W), LRU_BLOCK_W ** -0.5),
        'lru_b_gate': nrm((nC, 2, LRU_BLOCKS, 2 * LRU_BLOCK_W), 0.02),
        'lru_lambda': jnp.log(lru_a) - jnp.log1p(-lru_a),
        'lru_w_out': nrm((nC, LRU_WIDTH, D), LRU_WIDTH ** -0.5),
        'swa_w_in': nrm((nD, D, SWA_IN_DIM), D ** -0.5),
        'swa_sink': nrm((nD, SWA_Q_HEADS), 0.5),
        'swa_w_out': nrm((nD, SWA_Q_DIM, D), SWA_Q_DIM ** -0.5),
    }


def reference(x, c, ctx, c_ctx, norm_mix_g, norm_ffn_g, w_mod, b_mod, w_ff1, w_ff2, norm_out_g,
              gdn_w_in, gdn_conv_w, gdn_a_log, gdn_dt_bias, gdn_norm_g, gdn_w_out,
              ret_w_in, ret_decay_logit, ret_norm_g, ret_w_out,
              lru_w_in, lru_conv_w, lru_conv_b, lru_w_gate, lru_b_gate, lru_lambda, lru_w_out,
              swa_w_in, swa_sink, swa_w_out):
    n_tok = x.shape[1]
    rows = n_tok // GRID_W
    ang_swa = axial_angles(rows, SWA_HEAD_DIM)
    ang_ret = axial_angles(rows, RET_DK)
    xl, xc = x, ctx
    for i in range(DEPTH):
        kind, j = i % N_MIXERS, i // N_MIXERS
        need_ctx = i < DEPTH - 1
        ml = [m[:, None, :] for m in adaln(c, w_mod[i], b_mod[i])]
        mc = adaln(c_ctx, w_mod[i], b_mod[i])
        hl = modulate(rmsnorm(xl, norm_mix_g[i]), ml[0], ml[1])
        hc = modulate(rmsnorm(xc, norm_mix_g[i]), mc[0], mc[1])
        if kind == 0:
            yc, yl = gdn_mixer(hc, hl, gdn_w_in[j], gdn_conv_w[j], gdn_a_log[j], gdn_dt_bias[j],
                               gdn_norm_g[j], gdn_w_out[j], need_ctx)
        elif kind == 1:
            yc, yl = retention_mixer(hc, hl, ret_w_in[j], ret_decay_logit[j], ret_norm_g[j],
                                     ret_w_out[j], ang_ret, need_ctx)
        elif kind == 2:
            yc, yl = rglru_mixer(hc, hl, lru_w_in[j], lru_conv_w[j], lru_conv_b[j], lru_w_gate[j],
                                 lru_b_gate[j], lru_lambda[j], lru_w_out[j], need_ctx)
        else:
            yc, yl = swa_mixer(hc, hl, swa_w_in[j], swa_sink[j], swa_w_out[j], ang_swa, need_ctx)
        xl = xl + ml[2] * yl
        hl = modulate(rmsnorm(xl, norm_ffn_g[i]), ml[3], ml[4])
        xl = xl + ml[5] * sqrelu_mlp(hl, w_ff1[i], w_ff2[i])
        if need_ctx:
            xc = xc + mc[2] * yc
            hc = modulate(rmsnorm(xc, norm_ffn_g[i]), mc[3], mc[4])
            xc = xc + mc[5] * sqrelu_mlp(hc, w_ff1[i], w_ff2[i])
    return rmsnorm(xl, norm_out_g)
```

```python
import numpy as np
import concourse.bass as bass
import concourse.mybir as mybir
from concourse.bass_utils import run_bass_kernel_spmd

F32 = mybir.dt.float32
BF16 = mybir.dt.bfloat16
AF = mybir.ActivationFunctionType
ALU = mybir.AluOpType
AX = mybir.AxisListType

D = 1024
T = 2304
NCTX = 256
L = 2048
TB = [(0, 256), (256, 768), (768, 1280), (1280, 1792), (1792, 2304)]
EPS = 1e-6


class St:
    __slots__ = ("lw", "rd")

    def __init__(self):
        self.lw = None
        self.rd = []


class View:
    __slots__ = ("buf", "ap")

    def __init__(self, buf, ap):
        self.buf = buf
        self.ap = ap


class Buf:
    def __init__(self, t, states=None):
        self.t = t
        self.states = states if states is not None else [St()]

    def __getitem__(self, idx):
        return View(self, self.t[idx])

    def v(self, ap):
        return View(self, ap)

    def sub(self):
        st = St()
        if not hasattr(self, "_haskids"):
            self.states = []
            self._haskids = True
        self.states.append(st)
        return Buf(self.t, [st])


class Sched:
    ENG = ("pe", "act", "dve", "pool", "sp")

    def __init__(self, nc, n_dsem=10):
        self.nc = nc
        self.cnt = {e: 0 for e in self.ENG}
        self.sem = {}
        self.waited = {e: {} for e in self.ENG}
        self._ctx = []
        for e in self.ENG:
            if e == "sp":
                continue
            cm = nc.semaphore("s_" + e)
            self.sem[e] = cm.__enter__()
            self._ctx.append(cm)
        self.dsem = {}
        self.dcnt = {}
        self.dnext = {}
        for q in ("sp", "pool", "act"):
            lst = []
            for i in range(n_dsem):
                cm = nc.semaphore("d_%s%d" % (q, i))
                lst.append(cm.__enter__())
                self._ctx.append(cm)
            self.dsem[q] = lst
            self.dcnt[q] = [0] * n_dsem
            self.dnext[q] = 0
        self.ninstr = 0
        self.eobj = {"pe": nc.tensor, "act": nc.scalar, "dve": nc.vector, "pool": nc.gpsimd,
                     "sp": nc.sync}
        self.banks = []
        self.bank_i = 0

    def sb(self, name, shape, dt):
        self.uid = getattr(self, "uid", 0) + 1
        name = "sb%d_%s" % (self.uid, name)
        cm = self.nc.sbuf_tensor(name, list(shape), dt)
        t = cm.__enter__()
        self._ctx.append(cm)
        return Buf(t)

    def ps(self, name, shape, dt=F32):
        cm = self.nc.psum_tensor(name, list(shape), dt)
        t = cm.__enter__()
        self._ctx.append(cm)
        b = Buf(t)
        b.psum = True
        return b

    def bank(self):
        b = self.banks[self.bank_i]
        self.bank_i = (self.bank_i + 1) % len(self.banks)
        return b

    class _Scope:
        def __init__(self, s):
            self.s = s

        def __enter__(self):
            self.mark = len(self.s._ctx)
            return self

        def __exit__(self, *a):
            s = self.s
            s.barrier()
            while len(s._ctx) > self.mark:
                s._ctx.pop().__exit__(None, None, None)
            return False

    def scope(self):
        return Sched._Scope(self)

    def barrier(self):
        evs = []
        for e in ("pe", "act", "dve", "pool"):
            if self.cnt[e] > 0:
                evs.append((self.sem[e], self.cnt[e], e))
        for q in ("sp", "pool", "act"):
            for i, c in enumerate(self.dcnt[q]):
                if c > 0:
                    evs.append((self.dsem[q][i], c, "d_%s%d" % (q, i)))
        for e in self.ENG:
            self.wait_all(e, evs)

    def _need(self, eng, ev, waits):
        if ev is None:
            return
        sem, val, key = ev
        if self.waited[eng].get(key, 0) >= val:
            return
        cur = waits.get(key)
        if cur is None or cur[1] < val:
            waits[key] = (sem, val)

    def _deps(self, eng, outs, ins):
        waits = {}
        for v in ins:
            for st in v.buf.states:
                self._need(eng, st.lw, waits)
                if getattr(v.buf, "psum", False):
                    for r in st.rd:
                        if r[2] != eng:
                            self._need(eng, r, waits)
        for v in outs:
            for st in v.buf.states:
                self._need(eng, st.lw, waits)
                for r in st.rd:
                    self._need(eng, r, waits)
        wl = []
        for key, (sem, val) in waits.items():
            self.waited[eng][key] = val
            wl.append((sem, val))
        return wl

    def _commit(self, ev, outs, ins):
        for v in ins:
            for st in v.buf.states:
                st.rd.append(ev)
                if len(st.rd) > 24:
                    best = {}
                    for r in st.rd:
                        if r[2] not in best or best[r[2]][1] < r[1]:
                            best[r[2]] = r
                    st.rd = list(best.values())
        for v in outs:
            for st in v.buf.states:
                st.lw = ev
                st.rd = []

    def op(self, eng, outs, ins, fn):
        wl = self._deps(eng, outs, ins)
        self.cnt[eng] += 1
        sem = self.sem[eng]
        ev = (sem, self.cnt[eng], eng)
        self._commit(ev, outs, ins)
        self._emit_now(eng, wl, fn, sem, 1)
        self.ninstr += 1
        return ev

    def dma(self, q, out, in_, **kw):
        outs = [out] if isinstance(out, View) else []
        ins = [in_] if isinstance(in_, View) else []
        oap = out.ap if isinstance(out, View) else out
        iap = in_.ap if isinstance(in_, View) else in_
        wl = self._deps(q, outs, ins)
        i = self.dnext[q]
        self.dnext[q] = (i + 1) % len(self.dsem[q])
        sem = self.dsem[q][i]
        prev = self.dcnt[q][i]
        key = "d_%s%d" % (q, i)
        if prev > 0 and self.waited[q].get(key, 0) < prev:
            wl.append((sem, prev))
            self.waited[q][key] = prev
        self.dcnt[q][i] = prev + 16
        ev = (sem, prev + 16, key)
        self._commit(ev, outs, ins)
        self._emit_now(q, wl, lambda e: e.dma_start(out=oap, in_=iap, **kw), sem, 16)
        self.ninstr += 1
        return ev

    def wait_all(self, eng, evs):
        wl = []
        for ev in evs:
            sem, val, key = ev
            if self.waited[eng].get(key, 0) < val:
                self.waited[eng][key] = val
                wl.append((sem, val))
        self._emit_now(eng, wl, None, None, 0)

    def _emit_now(self, eng, wl, fn, sem, inc):
        e = self.eobj[eng]
        for (s, v) in wl:
            e.wait_ge(s, v)
        if fn is not None:
            fn(e).then_inc(sem, inc)

    def close(self):
        while self._ctx:
            self._ctx.pop().__exit__(None, None, None)

    def mm(self, out, lhsT, rhs, start=True, stop=True):
        ins = [lhsT, rhs]
        if not start:
            ins.append(out)
        return self.op("pe", [out], ins,
                       lambda e: e.matmul(out.ap, lhsT.ap, rhs.ap, start=start, stop=stop))

    def tr(self, out, in_, ident):
        return self.op("pe", [out], [in_, ident],
                       lambda e: e.transpose(out.ap, in_.ap, ident.ap))

    def act(self, out, in_, func, bias=None, scale=None, accum=None):
        ins = [in_]
        kw = {}
        if isinstance(bias, View):
            ins.append(bias)
            kw["bias"] = bias.ap
        elif bias is not None:
            kw["bias"] = bias
        if isinstance(scale, View):
            ins.append(scale)
            kw["scale"] = scale.ap
        elif scale is not None:
            kw["scale"] = scale
        outs = [out]
        if accum is not None:
            outs.append(accum)
            kw["accum_out"] = accum.ap
        return self.op("act", outs, ins, lambda e: e.activation(out.ap, in_.ap, func, **kw))

    def tt(self, eng, out, a, b, op):
        return self.op(eng, [out], [a, b], lambda e: e.tensor_tensor(out.ap, a.ap, b.ap, op))

    def ts(self, eng, out, a, s1, op0, s2=None, op1=None, accum=None):
        ins = [a]
        s1a = s1
        s2a = s2
        if isinstance(s1, View):
            ins.append(s1)
            s1a = s1.ap
        if isinstance(s2, View):
            ins.append(s2)
            s2a = s2.ap
        outs = [out]
        kw = {}
        if accum is not None:
            outs.append(accum)
            kw["accum_out"] = accum.ap
        if op1 is None:
            return self.op(eng, outs, ins,
                           lambda e: e.tensor_scalar(out.ap, a.ap, s1a, None, op0, **kw))
        return self.op(eng, outs, ins,
                       lambda e: e.tensor_scalar(out.ap, a.ap, s1a, s2a, op0, op1, **kw))

    def stt(self, eng, out, a, s, b, op0, op1):
        ins = [a, b]
        sa = s
        if isinstance(s, View):
            ins.append(s)
            sa = s.ap
        return self.op(eng, [out], ins,
                       lambda e: e.scalar_tensor_tensor(out.ap, a.ap, sa, b.ap, op0, op1))

    def copy(self, eng, out, a):
        if eng == "act":
            return self.op(eng, [out], [a], lambda e: e.copy(out.ap, a.ap))
        return self.op(eng, [out], [a], lambda e: e.tensor_copy(out.ap, a.ap))

    def memset(self, eng, out, val):
        return self.op(eng, [out], [], lambda e: e.memset(out.ap, val))

    def recip(self, out, a):
        return self.op("dve", [out], [a], lambda e: e.reciprocal(out.ap, a.ap))


class K:
    pass


def rr(ap, **kw):
    return ap.rearrange("(c p) n -> p c n", p=128, **kw)


def setup_common(nc, S, k, dram):
    k.S = S
    k.dram = dram
    S.banks = [S.ps("bank%d" % i, [128, 512], F32) for i in range(8)]
    k.xT = S.sb("xT", [128, 8, T], F32)
    k.hT = S.sb("hT", [128, 8, T], BF16)
    k.xTb = [k.xT.sub() for _ in TB]
    k.hTb = [k.hT.sub() for _ in TB]
    k.ident = S.sb("ident", [128, 128], F32)
    k.identb = S.sb("identb", [128, 128], BF16)
    k.onesm = S.sb("onesm", [128, 128], F32)
    k.ones = S.sb("ones", [128, 128], F32)
    k.onesb = S.sb("onesb", [128, 128], BF16)
    k.maskU = S.sb("maskU", [128, 128], F32)
    k.maskL = S.sb("maskL", [128, 128], F32)
    k.cst = S.sb("cst", [128, 8], F32)
    k.sc = S.sb("sc", [128, 8, 2], F32)
    k.mod = S.sb("mod", [128, 2, 48], F32)
    k.gn = S.sb("gn", [128, 9, 8], F32)
    k.bmod = S.sb("bmod", [128, 4, 48], F32)
    k.Amix = S.sb("Amix", [128, 2, 8], F32)
    k.Affn = S.sb("Affn", [128, 2, 8], F32)
    S.memset("dve", k.ones[:], 1.0)
    S.memset("dve", k.onesb[:], 1.0)
    S.memset("dve", k.onesm[:], 1.0 / D)
    S.memset("dve", k.cst[:, 0:1], EPS)
    S.memset("dve", k.cst[:, 1:2], 0.0)
    S.memset("dve", k.cst[:, 2:3], -np.pi)
    S.memset("dve", k.cst[:, 3:4], 1.0)
    S.memset("pool", k.ident[:], 1.0)
    S.op("pool", [k.ident[:]], [k.ident[:]], lambda e: e.affine_select(
        k.ident[:].ap, k.ident[:].ap, [[-1, 128]], ALU.is_equal, 0.0, base=0, channel_multiplier=1))
    S.copy("dve", k.identb[:], k.ident[:])
    S.memset("pool", k.maskU[:], 1.0)
    S.op("pool", [k.maskU[:]], [k.maskU[:]], lambda e: e.affine_select(
        k.maskU[:].ap, k.maskU[:].ap, [[1, 128]], ALU.is_ge, 0.0, base=0, channel_multiplier=-1))
    S.memset("pool", k.maskL[:], 1.0)
    S.op("pool", [k.maskL[:]], [k.maskL[:]], lambda e: e.affine_select(
        k.maskL[:].ap, k.maskL[:].ap, [[-1, 128]], ALU.is_ge, 0.0, base=0, channel_multiplier=1))
    S.dma("sp", k.xT[:], dram["xT"].rearrange("(c p) t -> p c t", p=128))
    S.dma("act", k.sc[:], dram["cs"])
    S.dma("act", k.gn[:], dram["gn"])
    S.dma("act", k.bmod[:], dram["bmod"])
    S.act(k.sc[:], k.sc[:], AF.Silu)


def adaln(k, i):
    S = k.S
    wm = k.dram["w_mod"]
    with S.scope():
        bufs = [S.sb("wm%d" % u, [128, 8, 512], F32) for u in range(2)]
        modps = S.bank()
        for fg in range(12):
            buf = bufs[fg % 2]
            S.dma("sp" if fg % 2 == 0 else "act", buf[:], rr(wm[k.lidx[i], :, fg * 512:(fg + 1) * 512]))
            for fc in range(4):
                f = fg * 4 + fc
                for kc in range(8):
                    S.mm(modps[:, 2 * f:2 * f + 2], buf[:, kc, fc * 128:(fc + 1) * 128],
                         k.sc[:, kc, :], start=(kc == 0), stop=(kc == 7))
        for r in range(2):
            S.tt("dve", k.mod[:, r, :], modps[:, r:96:2], k.bmod[:, i, :], ALU.add)
        for r in range(2):
            S.stt("dve", k.Amix[:, r, :], k.mod[:, r, 8:16], 1.0, k.gn[:, i, :], ALU.add, ALU.mult)
            S.stt("dve", k.Affn[:, r, :], k.mod[:, r, 32:40], 1.0, k.gn[:, 4 + i, :], ALU.add, ALU.mult)


def norm_mod(k, A, shift_off, blocks, tmp, rstd):
    S = k.S
    for b in blocks:
        a, e = TB[b]
        W = e - a
        r = 1 if b == 0 else 0
        ms = S.bank()
        for c in range(8):
            sq = tmp[c % 2]
            S.act(sq[:, :W], k.xTb[b][:, c, a:e], AF.Square)
            S.mm(ms[:, :W], k.onesm[:], sq[:, :W], start=(c == 0), stop=(c == 7))
        S.act(rstd[:, :W], ms[:, :W], AF.Sqrt, bias=k.cst[:, 0:1])
        S.recip(rstd[:, :W], rstd[:, :W])
        for c in range(8):
            t2 = tmp[2 + c % 2]
            S.stt("dve", t2[:, :W], k.xTb[b][:, c, a:e], A[:, r, c:c + 1], rstd[:, :W],
                  ALU.mult, ALU.mult)
            S.act(k.hTb[b][:, c, a:e], t2[:, :W], AF.Identity,
                  bias=k.mod[:, r, shift_off + c:shift_off + c + 1])


def ffn(k, i, blocks):
    S = k.S
    w1 = k.dram["w_ff1"]
    w2 = k.dram["w_ff2"]
    with S.scope():
        tmp = [S.sb("ntmp%d" % u, [128, 512], F32) for u in range(4)]
        rstd = S.sb("rstd", [128, 512], F32)
        norm_mod(k, k.Affn, 24, blocks, tmp, rstd)
        w1b = [S.sb("w1b%d" % u, [128, 8, 512], BF16) for u in range(2)]
        w2b = [S.sb("w2b%d" % u, [128, 4, 1024], BF16) for u in range(2)]
        a1 = [[S.sb("a1_%d_%d" % (u, b), [128, 4, TB[b][1] - TB[b][0]], BF16) for b in range(5)]
              for u in range(2)]
        for fg in range(8):
            u = fg % 2
            S.dma("pool", w1b[u][:], rr(w1[k.lidx[i], :, fg * 512:(fg + 1) * 512]))
            S.dma("pool", w2b[u][:], rr(w2[k.lidx[i], fg * 512:(fg + 1) * 512, :]))
            for b in blocks:
                a, e = TB[b]
                W = e - a
                for fc in range(4):
                    ps = S.bank()
                    for kc in range(8):
                        S.mm(ps[:, :W], w1b[u][:, kc, fc * 128:(fc + 1) * 128], k.hTb[b][:, kc, a:e],
                             start=(kc == 0), stop=(kc == 7))
                    t = tmp[fc % 4]
                    S.act(t[:, :W], ps[:, :W], AF.Relu)
                    S.tt("pool", a1[u][b][:, fc, :], t[:, :W], t[:, :W], ALU.mult)
            for b in blocks:
                a, e = TB[b]
                W = e - a
                r = 1 if b == 0 else 0
                for dc in range(8):
                    ps = S.bank()
                    for fc in range(4):
                        S.mm(ps[:, :W], w2b[u][:, fc, dc * 128:(dc + 1) * 128], a1[u][b][:, fc, :],
                             start=(fc == 0), stop=(fc == 3))
                    S.stt("dve", k.xTb[b][:, dc, a:e], ps[:, :W], k.mod[:, r, 40 + dc:41 + dc],
                          k.xTb[b][:, dc, a:e], ALU.mult, ALU.add)


def final_norm(k):
    S = k.S
    out = k.dram["out"].rearrange("(c p) t -> p c t", p=128)
    evs = []
    with S.scope():
        tmp = [S.sb("ftmp%d" % u, [128, 512], F32) for u in range(4)]
        rstd = S.sb("frstd", [128, 512], F32)
        for b in range(1, 5):
            a, e = TB[b]
            W = e - a
            ms = S.bank()
            for c in range(8):
                sq = tmp[c % 2]
                S.act(sq[:, :W], k.xTb[b][:, c, a:e], AF.Square)
                S.mm(ms[:, :W], k.onesm[:], sq[:, :W], start=(c == 0), stop=(c == 7))
            S.act(rstd[:, :W], ms[:, :W], AF.Sqrt, bias=k.cst[:, 0:1])
            S.recip(rstd[:, :W], rstd[:, :W])
            for c in range(8):
                t2 = tmp[2 + c % 2]
                S.stt("dve", t2[:, :W], k.xTb[b][:, c, a:e], k.gn[:, 8, c:c + 1], rstd[:, :W],
                      ALU.mult, ALU.mult)
                evs.append(S.dma("sp", out[:, c, a - NCTX:e - NCTX], t2[:, :W]))
        S.wait_all("sp", evs)


def store_state(k):
    S = k.S
    ev = S.dma("sp", k.dram["xT_out"].rearrange("(c p) t -> p c t", p=128), k.xT[:])
    S.wait_all("sp", [ev])


def rope_tables(k, cosT, sinT, nf, mode):
    S = k.S
    theta_ln = float(np.log(10000.0))
    I32 = mybir.dt.int32
    with S.scope():
        pat = S.sb("rp_pat", [128, 3, 128], F32)
        t1 = S.sb("rp_t1", [128, 4], F32)
        rowp = S.sb("rp_row", [128, 32, 64], F32)
        colp = S.sb("rp_col", [128, 32, 64], F32)
        itmp = S.sb("rp_int", [128, L], I32)

        def iota(dst, pattern):
            S.op("pool", [dst], [], lambda e: e.iota(dst.ap, pattern, base=0, channel_multiplier=0,
                                                    allow_small_or_imprecise_dtypes=True))
        iota(pat.v(pat.t[:, 0, :].rearrange("p (a b) -> p a b", b=nf)), [[0, 128 // nf], [1, nf]])
        iota(pat.v(pat.t[:, 1, :].rearrange("p (a b c) -> p a b c", b=2, c=nf)),
             [[0, 128 // (2 * nf)], [1, 2], [0, nf]])
        iota(pat.v(pat.t[:, 2, :].rearrange("p (a b c) -> p a b c", b=2, c=32)), [[0, 2], [1, 2], [0, 32]])
        for j in range(3):
            ps = S.bank()
            S.tr(ps[:, 0:128], pat[:, j, :], k.ident[:])
            S.copy("dve", t1[:, j:j + 1], ps[:, 0:1])
        iota(rowp[:], [[1, 32], [0, 64]])
        iota(colp[:], [[0, 32], [1, 64]])
        S.act(t1[:, 0:1], t1[:, 0:1], AF.Exp, scale=-theta_ln / nf)
        S.ts("dve", t1[:, 1:2], t1[:, 1:2], 2.0, ALU.mult, -1.0, ALU.add)
        if mode == "swa":
            S.tt("dve", colp[:], colp[:], rowp[:], ALU.subtract)
            S.stt("dve", rowp[:], colp[:], t1[:, 2:3], rowp[:], ALU.mult, ALU.add)
            pos = rowp
        elif mode == "row":
            pos = rowp
        else:
            pos = colp
        posf = pos.v(pos.t.rearrange("p a b -> p (a b)"))
        tmpb = colp if pos is rowp else rowp
        tmpf = tmpb.v(tmpb.t.rearrange("p a b -> p (a b)"))
        inv2pi = float(1.0 / (2 * np.pi))
        S.ts("dve", posf, posf, t1[:, 0:1], ALU.mult, inv2pi, ALU.mult)

        def sin_turns(dst, off):
            if off != 0.0:
                S.ts("dve", tmpf, posf, off, ALU.add)
                src = tmpf
            else:
                src = posf
            S.copy("dve", itmp[:], src)
            S.copy("dve", dst, itmp[:])
            S.tt("dve", dst, src, dst, ALU.subtract)
            S.ts("dve", tmpf, dst, 0.5, ALU.is_gt)
            S.tt("dve", dst, dst, tmpf, ALU.subtract)
            S.act(dst, dst, AF.Sin, scale=float(2 * np.pi))

        sin_turns(sinT[:], 0.0)
        S.ts("dve", sinT[:], sinT[:], t1[:, 1:2], ALU.mult)
        sin_turns(cosT[:], 0.25)


def swa_mixer(k, i):
    S = k.S
    dram = k.dram
    w_in = dram["swa_w_in"]
    w_rot = dram["swa_w_rot"]
    w_out = dram["swa_w_out"]
    gate_off = 16
    with S.scope():
        tmp = [S.sb("ntmp%d" % u, [128, 512], F32) for u in range(4)]
        rstd = S.sb("rstd", [128, 512], F32)
        norm_mod(k, k.Amix, 0, range(5), tmp, rstd)
        cosT = S.sb("cosT", [128, L], F32)
        sinT = S.sb("sinT", [128, L], F32)
        rope_tables(k, cosT, sinT, 16, "swa")
        esink = S.sb("esink", [128, 16], F32)
        S.dma("sp", esink[:], dram["swa_sink"])
        S.act(esink[:], esink[:], AF.Exp)
        mbp = S.sb("mbp", [128, 2, 128], BF16)
        mbn = S.sb("mbn", [128, 2, 128], BF16)
        for u in range(2):
            S.ts("dve", mbp[:, u, :], k.maskL[:], -1.0, ALU.add, 30000.0, ALU.mult)
            S.ts("dve", mbn[:, u, :], k.maskU[:], -1.0, ALU.add, 30000.0, ALU.mult)
        woutb = S.sb("woutb", [128, 8, 1024], BF16)
        S.dma("pool", woutb[:], rr(w_out))
        wq = [S.sb("wq%d" % u, [128, 8, 128], BF16) for u in range(2)]
        wqr = [S.sb("wqr%d" % u, [128, 8, 128], BF16) for u in range(2)]
        wv = S.sb("wv", [128, 8, 64], BF16)
        qT = S.sb("qT", [128, 2, L], BF16)
        kd = S.sb("kd", [128, T], BF16)
        vaug = S.sb("vaug", [128, 18, 65], BF16)
        PT = [S.sb("PT%d" % u, [128, 5, 512], BF16) for u in range(2)]
        ogs = [S.sb("og%d" % u, [128, 256], F32) for u in range(2)]
        oTg = S.sb("oTg", [128, 2, L], BF16)
        den = S.sb("den", [128, 4], F32)
        S.memset("dve", vaug[:, :, 64:65], 1.0)

        def rope_evac(dst, ps1, ps2, a, e):
            W = e - a
            ta = tmp[0]
            tb_ = tmp[1]
            S.tt("dve", ta[:, :W], ps1[:, :W], cosT[:, a - NCTX:e - NCTX], ALU.mult)
            S.tt("dve", tb_[:, :W], ps2[:, :W], sinT[:, a - NCTX:e - NCTX], ALU.mult)
            S.tt("pool", dst, ta[:, :W], tb_[:, :W], ALU.add)

        for g in range(4):
            for c2 in range(2):
                c = 2 * g + c2
                u = c2
                S.dma("pool", wq[u][:], rr(w_in[:, c * 128:(c + 1) * 128]))
                S.dma("pool", wqr[u][:], rr(w_rot[:, c * 128:(c + 1) * 128]))
                for b in range(1, 5):
                    a, e = TB[b]
                    ps1 = S.bank()
                    ps2 = S.bank()
                    for kc in range(8):
                        S.mm(ps1[:, :], wq[u][:, kc, :], k.hTb[b][:, kc, a:e], start=(kc == 0), stop=(kc == 7))
                    for kc in range(8):
                        S.mm(ps2[:, :], wqr[u][:, kc, :], k.hTb[b][:, kc, a:e], start=(kc == 0), stop=(kc == 7))
                    rope_evac(qT[:, c2, a - NCTX:e - NCTX], ps1, ps2, a, e)
            u = 0
            for half in range(2):
                S.dma("pool", wq[u][:, :, half * 64:(half + 1) * 64],
                      rr(w_in[:, 1024 + g * 64:1024 + (g + 1) * 64]))
                S.dma("pool", wqr[u][:, :, half * 64:(half + 1) * 64],
                      rr(w_rot[:, 1024 + g * 64:1024 + (g + 1) * 64]))
            S.dma("pool", wv[:], rr(w_in[:, 1280 + g * 64:1280 + (g + 1) * 64]))
            for b in range(5):
                a, e = TB[b]
                W = e - a
                ps1 = S.bank()
                for kc in range(8):
                    S.mm(ps1[:, :W], wq[u][:, kc, :], k.hTb[b][:, kc, a:e], start=(kc == 0), stop=(kc == 7))
                if b == 0:
                    S.copy("act", kd[:, a:e], ps1[:, :W])
                else:
                    ps2 = S.bank()
                    for kc in range(8):
                        S.mm(ps2[:, :W], wqr[u][:, kc, :], k.hTb[b][:, kc, a:e], start=(kc == 0), stop=(kc == 7))
                    rope_evac(kd[:, a:e], ps1, ps2, a, e)
            for t in range(18):
                b = 0 if t < 2 else 1 + (t - 2) // 4
                ps1 = S.bank()
                for kc in range(8):
                    S.mm(ps1[:, 0:64], k.hTb[b][:, kc, t * 128:(t + 1) * 128], wv[:, kc, :],
                         start=(kc == 0), stop=(kc == 7))
                S.copy("act", vaug[:, t, 0:64], ps1[:, 0:64])
            for qb in range(16):
                tiles = [(0, None), (1, None)]
                if qb > 0:
                    tiles.append((2 + qb - 1, mbp))
                tiles.append((2 + qb, None))
                if qb < 15:
                    tiles.append((2 + qb + 1, mbn))
                P = PT[qb % 2]
                og = ogs[qb % 2]
                for ti, (kt, mb) in enumerate(tiles):
                    ps = S.bank()
                    for half in range(2):
                        o = ps[:, half * 256:(half + 1) * 256]
                        if mb is not None:
                            S.mm(o, k.identb[:], mb[:, :, :], start=True, stop=False)
                        S.mm(o, kd[half * 64:(half + 1) * 64, kt * 128:(kt + 1) * 128],
                             qT[half * 64:(half + 1) * 64, :, qb * 128:(qb + 1) * 128],
                             start=(mb is None), stop=True)
                    S.act(P[:, ti, :], ps[:, :], AF.Exp, scale=0.125)
                po = S.bank()
                nt = len(tiles)
                for hh in range(4):
                    for ti, (kt, mb) in enumerate(tiles):
                        S.mm(po[:, hh * 65:(hh + 1) * 65], P[:, ti, hh * 128:(hh + 1) * 128], vaug[:, kt, :],
                             start=(ti == 0), stop=(ti == nt - 1))
                S.tt("dve", den[:], po[:, 64:260:65], esink[:, g * 4:(g + 1) * 4], ALU.add)
                S.recip(den[:], den[:])
                for hh in range(4):
                    half, c2 = hh // 2, hh % 2
                    hl = 2 * c2 + half
                    S.act(og[:, hl * 64:(hl + 1) * 64], po[:, hh * 65:hh * 65 + 64], AF.Identity,
                          scale=den[:, hh:hh + 1])
                for c2 in range(2):
                    ps = S.bank()
                    S.tr(ps[:, 0:128], og[:, c2 * 128:(c2 + 1) * 128], k.ident[:])
                    S.copy("act" if c2 == 0 else "dve", oTg[:, c2, qb * 128:(qb + 1) * 128], ps[:, 0:128])
            for b in range(1, 5):
                a, e = TB[b]
                for dc in range(8):
                    ps = S.bank()
                    for c2 in range(2):
                        S.mm(ps[:, :], woutb[:, 2 * g + c2, dc * 128:(dc + 1) * 128],
                             oTg[:, c2, a - NCTX:e - NCTX], start=(c2 == 0), stop=(c2 == 1))
                    S.stt("dve", k.xTb[b][:, dc, a:e], ps[:, :], k.mod[:, 0, gate_off + dc:gate_off + dc + 1],
                          k.xTb[b][:, dc, a:e], ALU.mult, ALU.add)


def lru_mixer(k, i):
    S = k.S
    dram = k.dram
    w_in = dram["lru_w_in"]
    w_out = dram["lru_w_out"]
    w_gate = dram["lru_w_gate"]
    gate_off = 16
    GC = float(2.0 * np.sqrt(2.0 / np.pi))
    with S.scope():
        with S.scope():
            tmp = [S.sb("ntmp%d" % u, [128, 512], F32) for u in range(4)]
            rstd = S.sb("rstd", [128, 512], F32)
            norm_mod(k, k.Amix, 0, range(5), tmp, rstd)
        cw = S.sb("lru_cw", [128, 10, 4], F32)
        cb = S.sb("lru_cb", [128, 10], F32)
        bg = S.sb("lru_bg", [128, 2, 10, 2], F32)
        c8 = S.sb("lru_c8", [128, 2, 10], F32)
        S.dma("sp", cw[:], dram["lru_cw"])
        S.dma("sp", cb[:], dram["lru_cb"])
        S.dma("sp", bg[:], dram["lru_bg"])
        S.dma("sp", c8[:], dram["lru_lam"])
        S.act(c8[:], c8[:], AF.Exp, scale=-1.0)
        S.act(c8[:], c8[:], AF.Ln, bias=k.cst[:, 3:4])
        S.ts("dve", c8[:], c8[:], -8.0, ALU.mult)
        c16 = S.sb("lru_c16", [128, 2, 10], F32)
        S.ts("dve", c16[:], c8[:], 2.0, ALU.mult)
        wx = [S.sb("lru_wx%d" % u, [128, 8, 128], BF16) for u in range(2)]
        wg = [S.sb("lru_wg%d" % u, [128, 8, 128], BF16) for u in range(2)]
        wgt = [S.sb("lru_wgt%d" % u, [128, 2, 256], BF16) for u in range(2)]
        wo = [S.sb("lru_wo%d" % u, [128, 1024], BF16) for u in range(2)]
        xpc = S.sb("lru_xpc", [128, NCTX + 3], F32)
        xpl = S.sb("lru_xpl", [128, L + 3], F32)
        B = [S.sb("lru_B%d" % u, [128, T], F32) for u in range(6)]
        xsb = S.sb("lru_xsb", [128, T], BF16)
        mb = S.sb("lru_mb", [128, T], BF16)
        S.memset("dve", xpc[:], 0.0)
        S.memset("dve", xpl[:], 0.0)
        SEG = [(xpc, 0, NCTX), (xpl, NCTX, T)]
        for n in range(10):
            u = n % 2
            S.dma("pool", wx[u][:], rr(w_in[:, n * 128:(n + 1) * 128]))
            S.dma("pool", wg[u][:], rr(w_in[:, 1280 + n * 128:1280 + (n + 1) * 128]))
            S.dma("pool", wgt[u][:], w_gate[:, n, :, :].rearrange("d c e -> c d e"))
            S.dma("pool", wo[u][:], w_out[n * 128:(n + 1) * 128, :])
            xs, Br, Bi, Bt, Bh, Bg = B
            for b in range(5):
                a, e = TB[b]
                W = e - a
                ps = S.bank()
                for kc in range(8):
                    S.mm(ps[:, :W], wx[u][:, kc, :], k.hTb[b][:, kc, a:e], start=(kc == 0), stop=(kc == 7))
                if b == 0:
                    S.copy("act", xpc[:, 2:2 + W], ps[:, :W])
                else:
                    S.copy("act", xpl[:, 2 + a - NCTX:2 + e - NCTX], ps[:, :W])
                ps2 = S.bank()
                for kc in range(8):
                    S.mm(ps2[:, :W], wg[u][:, kc, :], k.hTb[b][:, kc, a:e], start=(kc == 0), stop=(kc == 7))
                S.copy("dve", Bg[:, a:e], ps2[:, :W])
            for (xp, a, e) in SEG:
                W = e - a
                S.ts("dve", xs[:, a:e], xp[:, 0:W], cw[:, n, 0:1], ALU.mult, cb[:, n:n + 1], ALU.add)
                for j in range(1, 4):
                    S.stt("dve", xs[:, a:e], xp[:, j:j + W], cw[:, n, j:j + 1], xs[:, a:e], ALU.mult, ALU.add)
            S.copy("pool", xsb[:], xs[:])
            for d in range(2):
                for b in range(5):
                    a, e = TB[b]
                    W = e - a
                    ps = S.bank()
                    S.mm(ps[:, :W], wgt[u][:, d, 0:128], xsb[:, a:e])
                    S.act(Br[:, a:e], ps[:, :W], AF.Sigmoid, bias=bg[:, d, n, 0:1])
                    ps2 = S.bank()
                    S.mm(ps2[:, :W], wgt[u][:, d, 128:256], xsb[:, a:e])
                    S.act(Bi[:, a:e], ps2[:, :W], AF.Sigmoid, bias=bg[:, d, n, 1:2])
                S.act(Bt[:], Br[:], AF.Exp, scale=c16[:, d, n:n + 1])
                S.act(Bt[:], Bt[:], AF.Sqrt, bias=k.cst[:, 3:4], scale=-1.0)
                S.act(Br[:], Br[:], AF.Exp, scale=c8[:, d, n:n + 1])
                S.tt("dve", Bi[:], Bi[:], Bt[:], ALU.mult)
                S.tt("pool", Bi[:], Bi[:], xs[:], ALU.mult)
                hd = Bh if d == 0 else Bt
                if d == 0:
                    S.op("dve", [hd[:, 0:NCTX]], [Br[:, 0:NCTX], Bi[:, 0:NCTX]],
                         lambda e_: e_.tensor_tensor_scan(hd[:, 0:NCTX].ap, Br[:, 0:NCTX].ap, Bi[:, 0:NCTX].ap,
                                                          0.0, ALU.mult, ALU.add))
                    S.op("dve", [hd[:, NCTX:T]], [Br[:, NCTX:T], Bi[:, NCTX:T], hd[:, NCTX - 1:NCTX]],
                         lambda e_: e_.tensor_tensor_scan(hd[:, NCTX:T].ap, Br[:, NCTX:T].ap, Bi[:, NCTX:T].ap,
                                                          hd[:, NCTX - 1:NCTX].ap, ALU.mult, ALU.add))
                else:
                    def rv(bf, a, e):
                        return bf.t[:, a:e][:, ::-1]
                    S.op("dve", [hd[:, 0:NCTX]], [Br[:, 0:NCTX], Bi[:, 0:NCTX]],
                         lambda e_: e_.tensor_tensor_scan(rv(hd, 0, NCTX), rv(Br, 0, NCTX), rv(Bi, 0, NCTX),
                                                          0.0, ALU.mult, ALU.add))
                    S.op("dve", [hd[:, NCTX:T]], [Br[:, NCTX:T], Bi[:, NCTX:T], hd[:, 0:1]],
                         lambda e_: e_.tensor_tensor_scan(rv(hd, NCTX, T), rv(Br, NCTX, T), rv(Bi, NCTX, T),
                                                          hd[:, 0:1].ap, ALU.mult, ALU.add))
                    S.tt("pool", Bh[:], Bh[:], Bt[:], ALU.add)
            S.act(Bt[:], Bg[:], AF.Square)
            S.ts("dve", Bt[:], Bt[:], 0.044715, ALU.mult, 1.0, ALU.add)
            S.tt("pool", Bt[:], Bt[:], Bg[:], ALU.mult)
            S.act(Bt[:], Bt[:], AF.Sigmoid, scale=GC)
            S.tt("pool", Bt[:], Bt[:], Bg[:], ALU.mult)
            S.tt("dve", mb[:], Bt[:], Bh[:], ALU.mult)
            for b in range(5):
                a, e = TB[b]
                W = e - a
                r = 1 if b == 0 else 0
                for dc in range(8):
                    ps = S.bank()
                    S.mm(ps[:, :W], wo[u][:, dc * 128:(dc + 1) * 128], mb[:, a:e])
                    S.stt("dve", k.xTb[b][:, dc, a:e], ps[:, :W], k.mod[:, r, gate_off + dc:gate_off + dc + 1],
                          k.xTb[b][:, dc, a:e], ALU.mult, ALU.add)


def bidx(n):
    return 0 if n < 2 else 1 + (n - 2) // 4


def rope_compact(k, nf, cr, sr, cc, sc_):
    S = k.S
    theta_ln = float(np.log(10000.0))
    I32 = mybir.dt.int32
    with S.scope():
        pat = S.sb("rc_pat", [128, 2, 128], F32)
        t1 = S.sb("rc_t1", [128, 2], F32)
        pos = S.sb("rc_pos", [128, 96], F32)
        u = S.sb("rc_u", [128, 96], F32)
        u2 = S.sb("rc_u2", [128, 96], F32)
        fr = S.sb("rc_fr", [128, 96], F32)
        it = S.sb("rc_it", [128, 96], I32)

        def iota(dst, pattern):
            S.op("pool", [dst], [], lambda e: e.iota(dst.ap, pattern, base=0, channel_multiplier=0,
                                                    allow_small_or_imprecise_dtypes=True))
        iota(pat.v(pat.t[:, 0, :].rearrange("p (a b) -> p a b", b=nf)), [[0, 128 // nf], [1, nf]])
        iota(pat.v(pat.t[:, 1, :].rearrange("p (a b c) -> p a b c", b=2, c=nf)),
             [[0, 128 // (2 * nf)], [1, 2], [0, nf]])
        for j in range(2):
            ps = S.bank()
            S.tr(ps[:, 0:128], pat[:, j, :], k.ident[:])
            S.copy("dve", t1[:, j:j + 1], ps[:, 0:1])
        iota(pos[:, 0:32], [[1, 32]])
        iota(pos[:, 32:96], [[1, 64]])
        S.act(t1[:, 0:1], t1[:, 0:1], AF.Exp, scale=-theta_ln / nf)
        S.ts("dve", t1[:, 1:2], t1[:, 1:2], 2.0, ALU.mult, -1.0, ALU.add)
        S.ts("dve", u[:], pos[:], t1[:, 0:1], ALU.mult, float(1.0 / (2 * np.pi)), ALU.mult)

        def sin_turns(dst_r, dst_c, off, signed):
            src = u
            if off != 0.0:
                S.ts("dve", u2[:], u[:], off, ALU.add)
                src = u2
            S.copy("dve", it[:], src[:])
            S.copy("dve", fr[:], it[:])
            S.tt("dve", fr[:], src[:], fr[:], ALU.subtract)
            S.ts("dve", pos[:], fr[:], 0.5, ALU.is_gt)
            S.tt("dve", fr[:], fr[:], pos[:], ALU.subtract)
            S.act(fr[:], fr[:], AF.Sin, scale=float(2 * np.pi))
            if signed:
                S.ts("dve", fr[:], fr[:], t1[:, 1:2], ALU.mult)
            S.copy("dve", dst_r, fr[:, 0:32])
            S.copy("dve", dst_c, fr[:, 32:96])

        sin_turns(cr[:], cc[:], 0.25, False)
        sin_turns(sr[:], sc_[:], 0.0, True)


def ret_mixer(k, i):
    S = k.S
    dram = k.dram
    w_in = dram["ret_w_in"]
    w_rot = dram["ret_w_rot"]
    w_out = dram["ret_w_out"]
    ofw = Buf(dram["ret_ofw"].tensor if hasattr(dram["ret_ofw"], "tensor") else dram["ret_ofw"])
    ofw_ap = dram["ret_ofw"]
    gate_off = 16
    with S.scope():
        with S.scope():
            tmp = [S.sb("ntmp%d" % u, [128, 512], F32) for u in range(4)]
            rstd = S.sb("rstd", [128, 512], F32)
            norm_mod(k, k.Amix, 0, range(5), tmp, rstd)
        cr = S.sb("ret_cr", [128, 32], F32)
        sr = S.sb("ret_sr", [128, 32], F32)
        cc = S.sb("ret_cc", [128, 64], F32)
        sc_ = S.sb("ret_sc", [128, 64], F32)
        rope_compact(k, 64, cr, sr, cc, sc_)
        lg = S.sb("ret_lg", [128, 8], F32)
        nlg = S.sb("ret_nlg", [128, 8], F32)
        gnorm = S.sb("ret_gn", [128, 16], F32)
        S.dma("sp", lg[:], dram["ret_decay"])
        S.dma("sp", gnorm[:], dram["ret_norm_g"])
        S.act(lg[:], lg[:], AF.Exp, scale=-1.0)
        S.act(nlg[:], lg[:], AF.Ln, bias=k.cst[:, 3:4])
        S.ts("dve", lg[:], nlg[:], -1.0, ALU.mult)
        DTm = S.sb("ret_DTm", [128, 128], F32)
        IF1 = S.sb("ret_IF1", [128, 128], F32)
        IFr = S.sb("ret_IFr", [128, 128], F32)
        PV = S.sb("ret_PV", [128, 2], F32)

        def iota(dst, pattern, base, cm):
            S.op("pool", [dst], [], lambda e: e.iota(dst.ap, pattern, base=base, channel_multiplier=cm,
                                                    allow_small_or_imprecise_dtypes=True))
        iota(DTm[:], [[1, 128]], 0, -1)
        iota(IF1[:], [[1, 128]], 1, 0)
        iota(IFr[:], [[-1, 128]], 128, 0)
        iota(PV[:, 0:1], [[0, 1]], 127, -1)
        iota(PV[:, 1:2], [[0, 1]], 0, 1)
        ET = S.sb("ret_ET", [128, 8, 128], F32)
        EG = S.sb("ret_EG", [128, 8, 128], BF16)
        EGf = S.sb("ret_EGf", [128, 128], F32)
        kdsc = S.sb("ret_kdsc", [128, 8], F32)
        gt = S.sb("ret_gt", [128, 8], F32)
        for d in range(2):
            for h in range(4):
                j = d * 4 + h
                if d == 0:
                    S.act(ET[:, j, :], DTm[:], AF.Exp, scale=lg[:, j:j + 1])
                    S.stt("dve", ET[:, j, :], ET[:, j, :], 0.0625, k.maskU[:], ALU.mult, ALU.mult)
                    S.act(EGf[:], IF1[:], AF.Exp, scale=lg[:, j:j + 1])
                    S.act(kdsc[:, j:j + 1], PV[:, 0:1], AF.Exp, scale=lg[:, j:j + 1])
                else:
                    S.act(ET[:, j, :], DTm[:], AF.Exp, scale=nlg[:, j:j + 1])
                    S.stt("dve", ET[:, j, :], ET[:, j, :], 0.0625, k.maskL[:], ALU.mult, ALU.mult)
                    S.act(EGf[:], IFr[:], AF.Exp, scale=lg[:, j:j + 1])
                    S.act(kdsc[:, j:j + 1], PV[:, 1:2], AF.Exp, scale=lg[:, j:j + 1])
                S.ts("dve", EG[:, j, :], EGf[:], 0.0625, ALU.mult)
        S.act(gt[:], lg[:], AF.Exp, scale=128.0)

        WA = S.sb("ret_WA", [128, 4096], BF16)
        WB = S.sb("ret_WB", [128, 4096], BF16)
        wq = WA.v(WA.t[:, 0:2048].rearrange("p (c n) -> p c n", n=256))
        wqr = WA.v(WA.t[:, 2048:4096].rearrange("p (c n) -> p c n", n=256))
        wo = WA.v(WA.t[:, :].rearrange("p (c n) -> p c n", n=1024))
        wv = WB.v(WB.t[:, :].rearrange("p (c n) -> p c n", n=512))
        qT = S.sb("ret_qT", [128, 2, T], BF16)
        kT = S.sb("ret_kT", [128, 2, T], BF16)
        vtok = S.sb("ret_vtok", [128, 18, 512], BF16)
        Sf = S.sb("ret_Sf", [128, 2, 512], F32)
        Sb = S.sb("ret_Sb", [128, 2, 512], BF16)
        PTb = [S.sb("ret_PT%d" % u, [128, 128], BF16) for u in range(2)]
        qd = [S.sb("ret_qd%d" % u, [128, 2, 128], BF16) for u in range(2)]
        kd = [S.sb("ret_kd%d" % u, [128, 256], BF16) for u in range(2)]
        otmp = [S.sb("ret_ot%d" % u, [128, 512], F32) for u in range(2)]
        ofr = [S.sb("ret_ofr%d" % u, [128, 512], F32) for u in range(2)]
        ftmp = [S.sb("ret_ft%d" % u, [128, 512], F32) for u in range(2)]
        yTb = [S.sb("ret_yT%d" % u, [128, 4, 128], BF16) for u in range(2)]
        st = S.sb("ret_st", [128, 8], F32)
        rta = S.sb("ret_rta", [128, 512], F32)
        rtb = S.sb("ret_rtb", [128, 512], F32)

        def proj_rope(dst, col0):
            S.dma("pool", wq, rr(w_in[:, col0:col0 + 256]))
            S.dma("pool", wqr, rr(w_rot[:, col0:col0 + 256]))
            for dkc in range(2):
                for b in range(5):
                    a, e = TB[b]
                    W = e - a
                    ps1 = S.bank()
                    for kc in range(8):
                        S.mm(ps1[:, :W], wq.buf.v(wq.ap[:, kc, dkc * 128:(dkc + 1) * 128]), k.hTb[b][:, kc, a:e],
                             start=(kc == 0), stop=(kc == 7))
                    if b == 0:
                        S.copy("act", dst[:, dkc, a:e], ps1[:, :W])
                        continue
                    ps2 = S.bank()
                    for kc in range(8):
                        S.mm(ps2[:, :W], wqr.buf.v(wqr.ap[:, kc, dkc * 128:(dkc + 1) * 128]), k.hTb[b][:, kc, a:e],
                             start=(kc == 0), stop=(kc == 7))
                    r0 = (a - NCTX) // 64
                    if dkc == 0:
                        cosv = cr.v(cr.t[:, r0:r0 + 8].unsqueeze(2).to_broadcast([128, 8, 64]))
                        sinv = sr.v(sr.t[:, r0:r0 + 8].unsqueeze(2).to_broadcast([128, 8, 64]))
                    else:
                        cosv = cc.v(cc.t[:, :].unsqueeze(1).to_broadcast([128, 8, 64]))
                        sinv = sc_.v(sc_.t[:, :].unsqueeze(1).to_broadcast([128, 8, 64]))
                    p1 = ps1.v(ps1.t[:, :].rearrange("p (a b) -> p a b", b=64))
                    p2 = ps2.v(ps2.t[:, :].rearrange("p (a b) -> p a b", b=64))
                    ta = rta.v(rta.t[:, :].rearrange("p (a b) -> p a b", b=64))
                    tb_ = rtb.v(rtb.t[:, :].rearrange("p (a b) -> p a b", b=64))
                    S.tt("dve", ta, p1, cosv, ALU.mult)
                    S.tt("dve", tb_, p2, sinv, ALU.mult)
                    S.tt("pool", dst[:, dkc, a:e], rta[:, :], rtb[:, :], ALU.add)

        import os as _os
        dbg = int(_os.environ.get("RET_DBG", "9"))
        for h in range(4 if dbg >= 9 else (2 if dbg >= 5 else 1)):
            if dbg < 1:
                break
            proj_rope(qT, h * 256)
            proj_rope(kT, 1024 + h * 256)
            S.dma("pool", wv, rr(w_in[:, 2048 + h * 512:2048 + (h + 1) * 512]))
            for t in range(18):
                b = bidx(t)
                ps = S.bank()
                for kc in range(8):
                    S.mm(ps[:, :], k.hTb[b][:, kc, t * 128:(t + 1) * 128], wv.buf.v(wv.ap[:, kc, :]),
                         start=(kc == 0), stop=(kc == 7))
                S.copy("act" if t % 2 == 0 else "dve", vtok[:, t, :], ps[:, :])
            if dbg < 2:
                continue
            S.dma("pool", wv, rr(w_in[:, 4096 + h * 512:4096 + (h + 1) * 512]))
            S.dma("pool", wo, rr(w_out[h * 512:(h + 1) * 512, :]))

            def finalize(n, po, rot):
                b = bidx(n)
                r = 1 if b == 0 else 0
                tok = slice(n * 128, (n + 1) * 128)
                fr_ = ofr[rot]
                S.dma("sp", fr_[:], ofw.v(ofw_ap[n]))
                ot = otmp[rot]
                S.tt("dve", ot[:], po[:, :], fr_[:], ALU.add)
                S.op("dve", [st[:, 0:1]], [ot[:]], lambda e: e.reduce_sum(st[:, 0:1].ap, ot[:].ap, axis=AX.X))
                S.ts("dve", st[:, 1:2], st[:, 0:1], -1.0 / 512, ALU.mult)
                S.act(ot[:], ot[:], AF.Identity, bias=st[:, 1:2])
                f0 = ftmp[0]
                S.act(f0[:], ot[:], AF.Square)
                S.op("dve", [st[:, 2:3]], [f0[:]], lambda e: e.reduce_sum(st[:, 2:3].ap, f0[:].ap, axis=AX.X))
                S.act(st[:, 3:4], st[:, 2:3], AF.Sqrt, bias=k.cst[:, 0:1], scale=1.0 / 512)
                S.recip(st[:, 4:5], st[:, 3:4])
                pg = S.bank()
                for kc in range(8):
                    S.mm(pg[:, :], k.hTb[b][:, kc, tok], wv.buf.v(wv.ap[:, kc, :]), start=(kc == 0), stop=(kc == 7))
                f1 = ftmp[1]
                S.act(f1[:], pg[:, :], AF.Silu)
                S.stt("dve", f1[:], ot[:], st[:, 4:5], f1[:], ALU.mult, ALU.mult)
                yT = yTb[rot]
                for ec in range(4):
                    pt = S.bank()
                    S.tr(pt[:, 0:128], f1[:, ec * 128:(ec + 1) * 128], k.ident[:])
                    S.act(yT[:, ec, :], pt[:, 0:128], AF.Identity, scale=gnorm[:, h * 4 + ec:h * 4 + ec + 1])
                for dc in range(8):
                    ps = S.bank()
                    for ec in range(4):
                        S.mm(ps[:, 0:128], wo.buf.v(wo.ap[:, ec, dc * 128:(dc + 1) * 128]), yT[:, ec, :],
                             start=(ec == 0), stop=(ec == 3))
                    S.stt("dve", k.xTb[b][:, dc, tok], ps[:, 0:128],
                          k.mod[:, r, gate_off + dc:gate_off + dc + 1], k.xTb[b][:, dc, tok], ALU.mult, ALU.add)

            for d in range(2 if dbg >= 4 else 1):
                if dbg < 3:
                    break
                j = d * 4 + h
                order = list(range(18)) if d == 0 else [1, 0] + list(range(17, 1, -1))
                first = True
                for idx, n in enumerate(order):
                    rot = idx % 2
                    tok = slice(n * 128, (n + 1) * 128)
                    pp = S.bank()
                    for dkc in range(2):
                        S.mm(pp[:, 0:128], kT[:, dkc, tok], qT[:, dkc, tok], start=(dkc == 0), stop=(dkc == 1))
                    S.tt("dve", PTb[rot][:], pp[:, 0:128], ET[:, j, :], ALU.mult)
                    if not first:
                        for dkc in range(2):
                            S.tt("pool", qd[rot][:, dkc, :], qT[:, dkc, tok], EG[:, j, :], ALU.mult)
                    last = (idx == len(order) - 1)
                    if not last:
                        for dkc in range(2):
                            pt = S.bank()
                            ptv = pt.v(pt.t[:, 0:64].bitcast(BF16))
                            S.tr(ptv, kT[:, dkc, tok], k.identb[:])
                            S.act(kd[rot][:, dkc * 128:(dkc + 1) * 128], ptv, AF.Identity, scale=kdsc[:, j:j + 1])
                    po = S.bank()
                    if not first:
                        for dkc in range(2):
                            S.mm(po[:, :], qd[rot][:, dkc, :], Sb[:, dkc, :], start=(dkc == 0), stop=False)
                    S.mm(po[:, :], PTb[rot][:], vtok[:, n, :], start=first, stop=True)
                    if d == 0:
                        S.copy("act", otmp[rot][:], po[:, :])
                        S.dma("sp", ofw.v(ofw_ap[n]), otmp[rot][:])
                    else:
                        finalize(n, po, rot)
                    if not last:
                        for dkc in range(2):
                            pS = S.bank()
                            S.mm(pS[:, :], kd[rot][:, dkc * 128:(dkc + 1) * 128], vtok[:, n, :])
                            if first:
                                S.copy("dve", Sf[:, dkc, :], pS[:, :])
                            else:
                                S.stt("dve", Sf[:, dkc, :], Sf[:, dkc, :], gt[:, j:j + 1], pS[:, :], ALU.mult, ALU.add)
                            S.copy("act", Sb[:, dkc, :], Sf[:, dkc, :])
                    first = False


def gdn_mixer(k, i):
    S = k.S
    dram = k.dram
    w_in = dram["gdn_w_in"]
    w_out = dram["gdn_w_out"]
    ofw_ap = dram["gdn_ofw"]
    ofw = Buf(ofw_ap)
    gate_off = 16
    NEG = -30000.0
    with S.scope():
        with S.scope():
            tmp = [S.sb("ntmp%d" % u, [128, 512], F32) for u in range(4)]
            rstd = S.sb("rstd", [128, 512], F32)
            norm_mod(k, k.Amix, 0, range(5), tmp, rstd)
        cw = S.sb("gdn_cw", [128, 32, 4], F32)
        gnorm = S.sb("gdn_gn", [128, 1], F32)
        dtb = S.sb("gdn_dtb", [128, 32], F32)
        nexpa = S.sb("gdn_nexpa", [128, 32], F32)
        S.dma("sp", cw[:], dram["gdn_cw"])
        S.dma("sp", gnorm[:], dram["gdn_norm_g"])
        S.dma("sp", dtb[:], dram["gdn_dtb"])
        S.dma("sp", nexpa[:], dram["gdn_alog"])
        S.act(nexpa[:], nexpa[:], AF.Exp)
        S.ts("dve", nexpa[:], nexpa[:], -1.0, ALU.mult)
        MBU = S.sb("gdn_MBU", [128, 128], F32)
        MBL = S.sb("gdn_MBL", [128, 128], F32)
        MSU = S.sb("gdn_MSU", [128, 128], F32)
        MSL = S.sb("gdn_MSL", [128, 128], F32)
        S.ts("dve", MBU[:], k.maskU[:], -1.0, ALU.add, -NEG, ALU.mult)
        S.ts("dve", MBL[:], k.maskL[:], -1.0, ALU.add, -NEG, ALU.mult)
        S.tt("dve", MSU[:], k.maskU[:], k.ident[:], ALU.subtract)
        S.tt("dve", MSL[:], k.maskL[:], k.ident[:], ALU.subtract)
        Rm = S.sb("gdn_Rm", [128, 3, 128], F32)
        RmT = S.sb("gdn_RmT", [128, 3, 128], F32)
        BD16 = S.sb("gdn_BD16", [128, 128], F32)
        OFFm = S.sb("gdn_OFF", [128, 3, 128], F32)
        for ii, bs in enumerate((16, 32, 64)):
            dst = Rm.v(Rm.t[:, ii, :].rearrange("p (a b) -> p a b", b=bs))
            S.op("pool", [dst], [], lambda e, dst=dst, bs=bs: e.iota(dst.ap, [[1, 128 // bs], [0, bs]], base=0,
                                                                   channel_multiplier=0,
                                                                   allow_small_or_imprecise_dtypes=True))
            ps = S.bank()
            S.tr(ps[:, 0:128], Rm[:, ii, :], k.ident[:])
            S.copy("dve", RmT[:, ii, :], ps[:, 0:128])
            S.tt("dve", Rm[:, ii, :], Rm[:, ii, :], RmT[:, ii, :], ALU.is_equal)
        S.copy("dve", BD16[:], Rm[:, 0, :])
        S.tt("dve", OFFm[:, 0, :], Rm[:, 1, :], Rm[:, 0, :], ALU.subtract)
        S.tt("dve", OFFm[:, 1, :], Rm[:, 2, :], Rm[:, 1, :], ALU.subtract)
        S.tt("dve", OFFm[:, 2, :], k.ones[:], Rm[:, 2, :], ALU.subtract)
        gtok = S.sb("gdn_gtok", [128, 18, 32], F32)
        btok = S.sb("gdn_btok", [128, 18, 32], F32)
        nbtok = S.sb("gdn_nbtok", [128, 18, 32], F32)
        wab = S.sb("gdn_wab", [128, 8, 64], BF16)
        xa = S.sb("gdn_xa", [128, 32], F32)
        S.dma("pool", wab[:], rr(w_in[:, 6144:6208]))
        for t in range(18):
            b = bidx(t)
            pab = S.bank()
            for kc in range(8):
                S.mm(pab[:, 0:64], k.hTb[b][:, kc, t * 128:(t + 1) * 128], wab[:, kc, :],
                     start=(kc == 0), stop=(kc == 7))
            for d in range(2):
                S.tt("dve", xa[:, d * 16:(d + 1) * 16], pab[:, d * 32:d * 32 + 16], dtb[:, d * 16:(d + 1) * 16],
                     ALU.add)
                S.act(btok[:, t, d * 16:(d + 1) * 16], pab[:, d * 32 + 16:d * 32 + 32], AF.Sigmoid)
            S.act(xa[:], xa[:], AF.Exp)
            S.act(xa[:], xa[:], AF.Ln, bias=k.cst[:, 3:4])
            S.tt("dve", gtok[:, t, :], xa[:], nexpa[:], ALU.mult)
        S.ts("dve", nbtok[:], btok[:], -1.0, ALU.mult)

        qT = S.sb("gdn_qT", [128, T], BF16)
        kT = S.sb("gdn_kT", [128, T], BF16)
        ktok = S.sb("gdn_ktok", [128, 18, 128], BF16)
        vtok = [S.sb("gdn_vtok%d" % u, [128, 18, 128], BF16) for u in range(2)]
        wz = S.sb("gdn_wz", [128, 8, 256], BF16)
        wo = S.sb("gdn_wo", [128, 2, 1024], BF16)
        wp = [S.sb("gdn_wp%d" % u, [128, 8, 128], BF16) for u in range(2)]
        wpi = [0]

        import os as _os
        _NJ = int(_os.environ.get("GDN_NJ", "8"))
        _ND = int(_os.environ.get("GDN_ND", "2"))
        _NI = int(_os.environ.get("GDN_NI", "18"))
        _CUT = int(_os.environ.get("GDN_CUT", "99"))
        for j in range(int(_os.environ.get('GDN_J0', '0')), _NJ):
            pair = (2 * j, 2 * j + 1)
            with S.scope():
                xpc = S.sb("gdn_xpc", [128, NCTX + 3], F32)
                xpl = S.sb("gdn_xpl", [128, L + 3], F32)
                tb = S.sb("gdn_tb", [128, T], F32)
                vT = S.sb("gdn_vT", [128, T], BF16)
                sq = [S.sb("gdn_sq%d" % u, [128, 512], F32) for u in range(2)]
                rn = [S.sb("gdn_rn%d" % u, [128, 512], F32) for u in range(2)]
                S.memset("dve", xpc[:], 0.0)
                S.memset("dve", xpl[:], 0.0)
                SEG = [(xpc, 0, NCTX), (xpl, NCTX, T)]

                def proj_conv(col, chunk):
                    w = wp[wpi[0] % 2]
                    wpi[0] += 1
                    S.dma("pool", w[:], rr(w_in[:, col:col + 128]))
                    for b in range(5):
                        a, e = TB[b]
                        W = e - a
                        ps = S.bank()
                        for kc in range(8):
                            S.mm(ps[:, :W], w[:, kc, :], k.hTb[b][:, kc, a:e], start=(kc == 0), stop=(kc == 7))
                        if b == 0:
                            S.copy("act", xpc[:, 2:2 + W], ps[:, :W])
                        else:
                            S.copy("act", xpl[:, 2 + a - NCTX:2 + e - NCTX], ps[:, :W])
                    for (xp, a, e) in SEG:
                        W = e - a
                        S.ts("dve", tb[:, a:e], xp[:, 0:W], cw[:, chunk, 0:1], ALU.mult)
                        for jj in range(1, 4):
                            S.stt("dve", tb[:, a:e], xp[:, jj:jj + W], cw[:, chunk, jj:jj + 1], tb[:, a:e],
                                  ALU.mult, ALU.add)
                    S.act(tb[:], tb[:], AF.Silu)

                def l2n(dst, scale):
                    for b in range(5):
                        a, e = TB[b]
                        W = e - a
                        u = b % 2
                        S.act(sq[u][:, :W], tb[:, a:e], AF.Square)
                        ps = S.bank()
                        S.mm(ps[:, :W], k.ones[:], sq[u][:, :W])
                        S.act(rn[u][:, :W], ps[:, :W], AF.Sqrt, bias=k.cst[:, 0:1])
                        S.recip(rn[u][:, :W], rn[u][:, :W])
                        S.stt("dve", dst[:, a:e], tb[:, a:e], scale, rn[u][:, :W], ALU.mult, ALU.mult)

                def to_tok(dst, srcT):
                    for t in range(18):
                        pt = S.bank()
                        ptv = pt.v(pt.t[:, 0:64].bitcast(BF16))
                        S.tr(ptv, srcT[:, t * 128:(t + 1) * 128], k.identb[:])
                        S.copy("act" if t % 2 == 0 else "dve", dst[:, t, :], ptv)

                proj_conv(j * 128, j)
                l2n(qT, float(128 ** -0.5))
                proj_conv(1024 + j * 128, 8 + j)
                l2n(kT, 1.0)
                to_tok(ktok, kT)
                for si, vh in enumerate(pair):
                    proj_conv(2048 + vh * 128, 16 + vh)
                    S.copy("pool", vT[:], tb[:])
                    to_tok(vtok[si], vT)
            S.dma("pool", wz[:], rr(w_in[:, 4096 + pair[0] * 128:4096 + (pair[1] + 1) * 128]))
            S.dma("pool", wo[:], rr(w_out[pair[0] * 128:(pair[1] + 1) * 128, :]))
            with S.scope():
                kkq = [S.sb("gdn_kkq%d" % u, [128, 256], F32) for u in range(2)]
                Sf = [S.sb("gdn_Sf%d" % u, [128, 128], F32) for u in range(2)]
                Sb = [S.sb("gdn_Sb%d" % u, [128, 128], BF16) for u in range(2)]
                yTb = [S.sb("gdn_yT%d" % u, [128, 128], BF16) for u in range(2)]
                fin = [S.sb("gdn_fin%d" % u, [128, 4, 128], F32) for u in range(2)]
                fst = [S.sb("gdn_fst%d" % u, [128, 4], F32) for u in range(2)]

                class SBufs:
                    pass
                SB_ = []
                NU = 1
                for si in range(2):
                    row = []
                    for u in range(NU):
                        o = SBufs()
                        nm = "gdn_s%d_%d_" % (si, u)
                        o.gU = S.sb(nm + "gU", [128, 128], F32)
                        o.sc = S.sb(nm + "sc", [128, 4], F32)
                        o.tD = S.sb(nm + "tD", [128, 128], F32)
                        o.ETi = S.sb(nm + "ETi", [128, 128], F32)
                        o.ETs = S.sb(nm + "ETs", [128, 128], F32)
                        o.EG = S.sb(nm + "EG", [128, 128], F32)
                        o.Pf = S.sb(nm + "Pf", [128, 128], F32)
                        o.Pb = [S.sb(nm + "Pb%d" % w, [128, 128], BF16) for w in range(2)]
                        o.PbT = [S.sb(nm + "PbT%d" % w, [128, 128], BF16) for w in range(2)]
                        o.PT = S.sb(nm + "PT", [128, 128], BF16)
                        o.PfT = S.sb(nm + "PfT", [128, 128], F32)
                        o.Poff = S.sb(nm + "Poff", [128, 2, 128], BF16)
                        o.PoffT = S.sb(nm + "PoffT", [128, 3, 128], BF16)
                        o.Yb = S.sb(nm + "Yb", [128, 2, 128], BF16)
                        o.Zb = S.sb(nm + "Zb", [128, 128], BF16)
                        o.ZTb = S.sb(nm + "ZTb", [128, 128], BF16)
                        o.B0k = S.sb(nm + "B0k", [128, 128], BF16)
                        o.YWT = S.sb(nm + "YWT", [128, 128], BF16)
                        o.U = S.sb(nm + "U", [128, 128], F32)
                        o.un = S.sb(nm + "un", [128, 128], BF16)
                        o.qd = S.sb(nm + "qd", [128, 128], BF16)
                        o.kd = S.sb(nm + "kd", [128, 128], BF16)
                        o.ot = S.sb(nm + "ot", [128, 128], F32)
                        row.append(o)
                    SB_.append(row)

                for d in range(_ND):
                    order = list(range(18)) if d == 0 else [1, 0] + list(range(17, 1, -1))
                    order = order[:_NI]
                    mask = k.maskU if d == 0 else k.maskL
                    MB = MBU if d == 0 else MBL
                    MS = MSU if d == 0 else MSL
                    lastc = 127 if d == 0 else 0
                    for idx, n in enumerate(order):
                        first = (idx == 0)
                        last = (idx == len(order) - 1)
                        tok = slice(n * 128, (n + 1) * 128)
                        b = bidx(n)
                        pk = S.bank()
                        S.mm(pk[:, 0:128], kT[:, tok], kT[:, tok])
                        S.mm(pk[:, 128:256], kT[:, tok], qT[:, tok])
                        KKQ = kkq[idx % 2]
                        S.copy("act", KKQ[:], pk[:, 0:256])
                        pos = []
                        for si, vh in enumerate(pair):
                            col = d * 16 + vh
                            o = SB_[si][idx % NU]
                            gcol = gtok[:, n, col:col + 1]
                            bcol = btok[:, n, col:col + 1]
                            nbcol = nbtok[:, n, col:col + 1]
                            S.ts("dve", o.gU[:], mask[:], gcol, ALU.mult)
                            pg = S.bank()
                            S.mm(pg[:, 0:128], k.ones[:], o.gU[:])
                            S.mm(pg[:, 128:256], o.gU[:], k.ones[:])
                            S.ts("dve", o.sc[:, 0:1], pg[:, 128:129], -1.0, ALU.mult)
                            S.act(o.sc[:, 1:2], pg[:, 128:129], AF.Exp)
                            if _CUT < 1:
                                continue
                            S.tt("dve", o.tD[:], pg[:, 0:128], MB[:], ALU.add)
                            S.act(o.ETi[:], o.tD[:], AF.Exp, bias=o.sc[:, 0:1])
                            S.act(o.EG[:], pg[:, 0:128], AF.Exp)
                            S.tt("pool", o.ETs[:], o.ETi[:], MS[:], ALU.mult)
                            S.stt("dve", o.Pf[:], KKQ[:, 0:128], bcol, o.ETs[:], ALU.mult, ALU.mult)
                            S.tt("pool", o.PT[:], KKQ[:, 128:256], o.ETi[:], ALU.mult)
                            if _CUT < 2:
                                continue
                            pt = S.bank()
                            S.tr(pt[:, 0:128], o.Pf[:], k.ident[:])
                            S.copy("act", o.PfT[:], pt[:, 0:128])
                            S.tt("pool", o.Pb[0][:], o.Pf[:], BD16[:], ALU.mult)
                            S.tt("pool", o.PbT[0][:], o.PfT[:], BD16[:], ALU.mult)
                            for li in range(2):
                                S.tt("pool", o.Poff[:, li, :], o.Pf[:], OFFm[:, li, :], ALU.mult)
                            for li in range(3):
                                S.tt("pool", o.PoffT[:, li, :], o.PfT[:], OFFm[:, li, :], ALU.mult)
                            S.tt("dve", o.Zb[:], k.ident[:], o.Pb[0][:], ALU.subtract)
                            S.tt("dve", o.ZTb[:], k.ident[:], o.PbT[0][:], ALU.subtract)
                            cur, curT = o.Pb[0], o.PbT[0]
                            if _CUT < 3:
                                continue
                            for lev in range(1, 4):
                                w = lev % 2
                                p2 = S.bank()
                                if lev < 3:
                                    S.mm(p2[:, 0:128], curT[:], cur[:])
                                S.mm(p2[:, 128:256], cur[:], curT[:])
                                if lev < 3:
                                    S.copy("act", o.Pb[w][:], p2[:, 0:128])
                                S.copy("act", o.PbT[w][:], p2[:, 128:256])
                                cur, curT = o.Pb[w], o.PbT[w]
                                pz = S.bank()
                                S.mm(pz[:, 0:128], curT[:], o.Zb[:])
                                S.mm(pz[:, 128:256], o.Zb[:], curT[:])
                                S.tt("dve", o.Zb[:], o.Zb[:], pz[:, 0:128], ALU.add)
                                S.tt("dve", o.ZTb[:], o.ZTb[:], pz[:, 128:256], ALU.add)
                            for li in range(3):
                                lastl = (li == 2)
                                pyy = S.bank()
                                S.mm(pyy[:, 0:128], o.PoffT[:, li, :], o.Zb[:])
                                if not lastl:
                                    S.mm(pyy[:, 128:256], o.Poff[:, li, :], o.ZTb[:])
                                S.copy("act", o.Yb[:, 0, :], pyy[:, 0:128])
                                if not lastl:
                                    S.copy("act", o.Yb[:, 1, :], pyy[:, 128:256])
                                pw = S.bank()
                                S.mm(pw[:, 0:128], o.ZTb[:], o.Yb[:, 0, :])
                                if not lastl:
                                    S.mm(pw[:, 128:256], o.Zb[:], o.Yb[:, 1, :])
                                S.tt("dve", o.Zb[:], o.Zb[:], pw[:, 0:128], ALU.subtract)
                                if not lastl:
                                    S.tt("dve", o.ZTb[:], o.ZTb[:], pw[:, 128:256], ALU.subtract)
                            if _CUT < 4:
                                continue
                            S.act(o.B0k[:], ktok[:, n, :], AF.Identity, scale=o.sc[:, 1:2])
                            py = S.bank()
                            S.mm(py[:, 0:128], o.B0k[:], o.Zb[:])
                            S.mm(py[:, 128:256], o.Zb[:], vtok[si][:, n, :])
                            S.copy("dve", o.YWT[:], py[:, 0:128])
                            S.act(o.U[:], py[:, 128:256], AF.Identity, scale=bcol)
                            if not first:
                                S.tt("pool", o.qd[:], qT[:, tok], o.EG[:], ALU.mult)
                            if not last:
                                S.act(o.kd[:], ktok[:, n, :], AF.Identity, scale=o.ETi[:, lastc:lastc + 1])
                            if _CUT < 5:
                                continue
                            if not first:
                                ps1 = S.bank()
                                S.mm(ps1[:, 0:128], o.YWT[:], Sb[si][:])
                                S.stt("dve", o.un[:], ps1[:, 0:128], nbcol, o.U[:], ALU.mult, ALU.add)
                            else:
                                S.copy("pool", o.un[:], o.U[:])
                            po = S.bank()
                            if not first:
                                S.mm(po[:, 0:128], o.qd[:], Sb[si][:], start=True, stop=False)
                            S.mm(po[:, 0:128], o.PT[:], o.un[:], start=first, stop=True)
                            S.copy("act", o.ot[:], po[:, 0:128])
                            if d == 0:
                                S.dma("sp", ofw.v(ofw_ap[si, n]), o.ot[:])
                            else:
                                pos.append(o.ot)
                                if _os.environ.get("GDN_DBGB", "0") == "1":
                                    S.dma("sp", ofw.v(ofw_ap[2 + si, n]), o.ot[:])
                            if not last:
                                pS = S.bank()
                                S.mm(pS[:, 0:128], o.kd[:], o.un[:])
                                if first:
                                    S.copy("dve", Sf[si][:], pS[:, 0:128])
                                else:
                                    S.stt("dve", Sf[si][:], Sf[si][:], o.EG[:, lastc:lastc + 1], pS[:, 0:128],
                                          ALU.mult, ALU.add)
                                S.copy("act", Sb[si][:], Sf[si][:])
                        if d == 1:
                            r = 1 if b == 0 else 0
                            for si, vh in enumerate(pair):
                                f = fin[si]
                                fs = fst[si]
                                S.dma("sp", f[:, 0, :], ofw.v(ofw_ap[si, n]))
                                S.tt("dve", f[:, 1, :], pos[si][:], f[:, 0, :], ALU.add)
                                S.act(f[:, 2, :], f[:, 1, :], AF.Square)
                                S.op("dve", [fs[:, 0:1]], [f[:, 2, :]],
                                     lambda e, fs=fs, f=f: e.reduce_sum(fs[:, 0:1].ap, f[:, 2, :].ap, axis=AX.X))
                                S.act(fs[:, 1:2], fs[:, 0:1], AF.Sqrt, bias=k.cst[:, 0:1], scale=1.0 / 128)
                                S.recip(fs[:, 2:3], fs[:, 1:2])
                                pz_ = S.bank()
                                for kc in range(8):
                                    S.mm(pz_[:, 0:128], k.hTb[b][:, kc, tok], wz[:, kc, si * 128:(si + 1) * 128],
                                         start=(kc == 0), stop=(kc == 7))
                                S.act(f[:, 3, :], pz_[:, 0:128], AF.Silu)
                                S.stt("dve", f[:, 3, :], f[:, 1, :], fs[:, 2:3], f[:, 3, :], ALU.mult, ALU.mult)
                                pt = S.bank()
                                S.tr(pt[:, 0:128], f[:, 3, :], k.ident[:])
                                S.act(yTb[si][:], pt[:, 0:128], AF.Identity, scale=gnorm[:, 0:1])
                            for dc in range(8):
                                ps = S.bank()
                                for si in range(2):
                                    S.mm(ps[:, 0:128], wo[:, si, dc * 128:(dc + 1) * 128], yTb[si][:],
                                         start=(si == 0), stop=(si == 1))
                                S.stt("dve", k.xTb[b][:, dc, tok], ps[:, 0:128],
                                      k.mod[:, r, gate_off + dc:gate_off + dc + 1], k.xTb[b][:, dc, tok],
                                      ALU.mult, ALU.add)


MIXERS = {0: gdn_mixer, 1: ret_mixer, 2: lru_mixer, 3: swa_mixer}


def build_program(layers, do_final, stages=("mix", "ffn")):
    nc = bass.Bass("TRN2", target_bir_lowering=False)

    def inp(name, shape):
        return nc.dram_tensor(name, list(shape), F32, kind="ExternalInput").ap()

    dram = {}
    dram["xT"] = inp("xT", [D, T])
    dram["cs"] = inp("cs", [128, 8, 2])
    dram["gn"] = inp("gn", [128, 9, 8])
    dram["bmod"] = inp("bmod", [128, 4, 48])
    nl = len(layers)
    dram["w_mod"] = inp("w_mod", [nl, D, 6 * D])
    dram["w_ff1"] = inp("w_ff1", [nl, D, 4 * D])
    dram["w_ff2"] = inp("w_ff2", [nl, 4 * D, D])
    if 3 in layers:
        dram["swa_w_in"] = inp("swa_w_in", [D, 1536])
        dram["swa_w_rot"] = inp("swa_w_rot", [D, 1280])
        dram["swa_w_out"] = inp("swa_w_out", [D, D])
        dram["swa_sink"] = inp("swa_sink", [128, 16])
    if 0 in layers:
        dram["gdn_w_in"] = inp("gdn_w_in", [D, 6208])
        dram["gdn_w_out"] = inp("gdn_w_out", [2048, D])
        dram["gdn_cw"] = inp("gdn_cw", [128, 32, 4])
        dram["gdn_norm_g"] = inp("gdn_norm_g", [128, 1])
        dram["gdn_dtb"] = inp("gdn_dtb", [128, 32])
        dram["gdn_alog"] = inp("gdn_alog", [128, 32])
        dram["gdn_ofw"] = nc.dram_tensor("gdn_ofw", [4, 18, 128, 128], F32).ap()
    if 1 in layers:
        dram["ret_w_in"] = inp("ret_w_in", [D, 6144])
        dram["ret_w_rot"] = inp("ret_w_rot", [D, 2048])
        dram["ret_w_out"] = inp("ret_w_out", [2048, D])
        dram["ret_decay"] = inp("ret_decay", [128, 8])
        dram["ret_norm_g"] = inp("ret_norm_g", [128, 16])
        dram["ret_ofw"] = nc.dram_tensor("ret_ofw", [18, 128, 512], F32).ap()
    if 2 in layers:
        dram["lru_w_in"] = inp("lru_w_in", [D, 2560])
        dram["lru_w_out"] = inp("lru_w_out", [1280, D])
        dram["lru_w_gate"] = inp("lru_w_gate", [2, 10, 128, 256])
        dram["lru_cw"] = inp("lru_cw", [128, 10, 4])
        dram["lru_cb"] = inp("lru_cb", [128, 10])
        dram["lru_bg"] = inp("lru_bg", [128, 2, 10, 2])
        dram["lru_lam"] = inp("lru_lam", [128, 2, 10])
    if do_final:
        dram["out"] = nc.dram_tensor("out", [D, L], F32, kind="ExternalOutput").ap()
    else:
        dram["xT_out"] = nc.dram_tensor("xT_out", [D, T], F32, kind="ExternalOutput").ap()
    S = Sched(nc)
    k = K()
    k.lidx = {l: p for p, l in enumerate(layers)}
    setup_common(nc, S, k, dram)
    for i in layers:
        adaln(k, i)
        if "mix" in stages:
            MIXERS[i](k, i)
        if "ffn" in stages:
            ffn(k, i, range(5) if i < 3 else range(1, 5))
    if do_final:
        final_norm(k)
    else:
        store_state(k)
    S.close()
    k.ninstr = S.ninstr
    return nc, k


def pl(v):
    v = np.asarray(v, np.float32)
    lead = v.shape[:-1]
    n = v.shape[-1] // 128
    w = v.reshape(lead + (n, 128))
    return np.ascontiguousarray(np.moveaxis(w, -1, 0))


def rope_perm(n_heads, dh):
    q = dh // 4
    idx = []
    for h in range(n_heads):
        base = h * dh
        for blk in (1, 0, 3, 2):
            idx.extend(range(base + blk * q, base + (blk + 1) * q))
    return np.array(idx)


def host_shared(inputs, layers):
    sh = {}
    sh["gn"] = np.ascontiguousarray(np.concatenate(
        [pl(inputs["norm_mix_g"]), pl(inputs["norm_ffn_g"]), pl(inputs["norm_out_g"][None])], axis=1))
    sh["bmod"] = pl(inputs["b_mod"])
    ls = list(layers)
    sh["w_mod"] = np.ascontiguousarray(np.asarray(inputs["w_mod"], np.float32)[ls])
    sh["w_ff1"] = np.ascontiguousarray(np.asarray(inputs["w_ff1"], np.float32)[ls])
    sh["w_ff2"] = np.ascontiguousarray(np.asarray(inputs["w_ff2"], np.float32)[ls])
    if 3 in layers:
        w_in = np.asarray(inputs["swa_w_in"][0], np.float32)
        sh["swa_w_in"] = np.ascontiguousarray(w_in)
        perm = np.concatenate([rope_perm(16, 64), 1024 + rope_perm(4, 64)])
        sh["swa_w_rot"] = np.ascontiguousarray(w_in[:, perm])
        sh["swa_w_out"] = np.ascontiguousarray(inputs["swa_w_out"][0], np.float32)
        sink = np.asarray(inputs["swa_sink"][0], np.float32)
        order = [4 * g + 2 * (hh % 2) + hh // 2 for g in range(4) for hh in range(4)]
        sh["swa_sink"] = np.ascontiguousarray(np.broadcast_to(sink[order][None, :], (128, 16)))
    if 0 in layers:
        sh["gdn_w_in"] = np.ascontiguousarray(inputs["gdn_w_in"][0], np.float32)
        sh["gdn_w_out"] = np.ascontiguousarray(inputs["gdn_w_out"][0], np.float32)
        cw = np.asarray(inputs["gdn_conv_w"][0], np.float32)
        sh["gdn_cw"] = np.ascontiguousarray(cw.reshape(4, 32, 128).transpose(2, 1, 0))
        sh["gdn_norm_g"] = np.ascontiguousarray(np.asarray(inputs["gdn_norm_g"][0], np.float32).reshape(128, 1))
        sh["gdn_dtb"] = np.ascontiguousarray(np.broadcast_to(
            np.asarray(inputs["gdn_dt_bias"][0], np.float32).reshape(1, 32), (128, 32)))
        sh["gdn_alog"] = np.ascontiguousarray(np.broadcast_to(
            np.asarray(inputs["gdn_a_log"][0], np.float32).reshape(1, 32), (128, 32)))
    if 1 in layers:
        w_in = np.asarray(inputs["ret_w_in"][0], np.float32)
        sh["ret_w_in"] = np.ascontiguousarray(w_in)
        perm = np.concatenate([rope_perm(4, 256), 1024 + rope_perm(4, 256)])
        sh["ret_w_rot"] = np.ascontiguousarray(w_in[:, perm])
        sh["ret_w_out"] = np.ascontiguousarray(inputs["ret_w_out"][0], np.float32)
        dl = np.asarray(inputs["ret_decay_logit"][0], np.float32).reshape(8)
        sh["ret_decay"] = np.ascontiguousarray(np.broadcast_to(dl[None, :], (128, 8)))
        sh["ret_norm_g"] = pl(inputs["ret_norm_g"][0])
    if 2 in layers:
        sh["lru_w_in"] = np.ascontiguousarray(inputs["lru_w_in"][0], np.float32)
        sh["lru_w_out"] = np.ascontiguousarray(inputs["lru_w_out"][0], np.float32)
        sh["lru_w_gate"] = np.ascontiguousarray(inputs["lru_w_gate"][0], np.float32)
        cw = np.asarray(inputs["lru_conv_w"][0], np.float32)
        sh["lru_cw"] = np.ascontiguousarray(cw.reshape(4, 10, 128).transpose(2, 1, 0))
        sh["lru_cb"] = np.ascontiguousarray(np.asarray(inputs["lru_conv_b"][0], np.float32).reshape(10, 128).T)
        bgate = np.asarray(inputs["lru_b_gate"][0], np.float32)
        sh["lru_bg"] = np.ascontiguousarray(bgate.reshape(2, 10, 2, 128).transpose(3, 0, 1, 2))
        lam = np.asarray(inputs["lru_lambda"][0], np.float32)
        sh["lru_lam"] = np.ascontiguousarray(lam.reshape(2, 10, 128).transpose(2, 0, 1))
    return sh


def host_core(inputs, b, xT_state=None):
    m = {}
    if xT_state is None:
        m["xT"] = np.ascontiguousarray(
            np.concatenate([inputs["ctx"][b], inputs["x"][b]], axis=0).T.astype(np.float32))
    else:
        m["xT"] = xT_state
    cs = np.stack([np.asarray(inputs["c"][b], np.float32), np.asarray(inputs["c_ctx"], np.float32)], axis=0)
    m["cs"] = np.ascontiguousarray(np.moveaxis(pl(cs), 1, 2))
    return m


def run_layers(inputs, layers, do_final, states, cores, stages=("mix", "ffn")):
    nc, k = build_program(layers, do_final, stages)
    sh = host_shared(inputs, layers)
    in_maps = []
    for ci, b in enumerate(cores):
        m = host_core(inputs, b, None if states is None else states[ci])
        m.update(sh)
        in_maps.append(m)
    res = run_bass_kernel_spmd(nc, in_maps, core_ids=list(range(len(cores))))
    key = "out" if do_final else "xT_out"
    return [r[key] for r in res.results]


def kernel(**inputs):
    inputs = {k_: np.asarray(v) for k_, v in inputs.items()}
    cores = list(range(8))
    outs = run_layers(inputs, [0, 1, 2, 3], True, None, cores)
    out = np.stack([s.T for s in outs], axis=0)
    return np.ascontiguousarray(out.astype(np.float32))
```
